# Optimizing a Trainium2 kernel written in Bass

```python
import math
import jax, jax.numpy as jnp
from jax import lax
import numpy as np

D_MODEL = 1024
BATCH = 1
SEQ = 16384
DEPTH = 2
DEC_BATCH = 32
DEC_SEQ = 64
PAST_LEN = 4096

CHUNK = 64
CONV_WIDTH = 4
HEAD_DIM = 64
GDN_WIDTH = D_MODEL // 4
GDN_HEADS = GDN_WIDTH // HEAD_DIM
SSM_WIDTH = D_MODEL // 2
SSM_HEADS = SSM_WIDTH // HEAD_DIM
SSM_GROUPS = 2
SSM_STATE = 128
RWKV_WIDTH = D_MODEL - GDN_WIDTH - SSM_WIDTH
RWKV_HEADS = RWKV_WIDTH // HEAD_DIM
RWKV_DECAY_LORA = 64
RWKV_ICLR_LORA = 64
RWKV_GATE_LORA = 128
MIX_WIDTH = GDN_WIDTH + SSM_WIDTH + RWKV_WIDTH
FFN_HIDDEN = -(-8 * D_MODEL // (3 * 256)) * 256
GDN_CONV_CH = 3 * GDN_WIDTH
GDN_COLS = GDN_CONV_CH + GDN_WIDTH + 2 * GDN_HEADS
SSM_BC = SSM_GROUPS * SSM_STATE
SSM_CONV_CH = SSM_WIDTH + 2 * SSM_BC
SSM_COLS = SSM_WIDTH + SSM_CONV_CH + SSM_HEADS
RWKV_COLS = 3 * RWKV_WIDTH + RWKV_DECAY_LORA + RWKV_ICLR_LORA + RWKV_GATE_LORA
IN_COLS = GDN_COLS + SSM_COLS + RWKV_COLS
NORM_EPS = 1e-6
RWKV_GN_EPS = 64e-5

kernel_name = 'hybrid_gdn_ssd_rwkv7_stream_step'


def rms_norm(x, w):
    xf = x.astype(jnp.float32)
    y = xf * lax.rsqrt(jnp.mean(xf * xf, axis=-1, keepdims=True) + NORM_EPS)
    return (y * w.astype(jnp.float32)).astype(x.dtype)


def l2_normalize(x):
    return x * lax.rsqrt(jnp.sum(x * x, axis=-1, keepdims=True) + NORM_EPS)


def causal_conv(x, prev, w):
    t = x.shape[1]
    xp = jnp.concatenate([prev.astype(x.dtype), x], axis=1)
    w = w.astype(x.dtype)
    y = xp[:, 0:t] * w[0]
    for j in range(1, CONV_WIDTH):
        y = y + xp[:, j:j + t] * w[j]
    return y, xp[:, t:]


def gated_delta_chunked(q, k, v, g, beta, s0, chunk):
    bsz, t, h, _ = q.shape
    dv = v.shape[-1]
    n = t // chunk

    def blocks(a):
        return jnp.moveaxis(a.reshape((bsz, n, chunk, h) + a.shape[3:]), 3, 2)

    qc, kc, vc = blocks(q), blocks(k), blocks(v)
    gc, bc = blocks(g), blocks(beta)
    gcum = jnp.cumsum(gc, axis=-1)
    causal = jnp.tril(jnp.ones((chunk, chunk), dtype=bool))
    strict = jnp.tril(jnp.ones((chunk, chunk), dtype=bool), -1)
    diff = gcum[..., :, None] - gcum[..., None, :]
    decay = jnp.where(causal, jnp.exp(jnp.where(causal, diff, 0.0)), 0.0)
    k_beta = kc * bc[..., None]
    v_beta = vc * bc[..., None]
    lower = jnp.where(strict, jnp.einsum('bnhik,bnhjk->bnhij', k_beta, kc) * decay, 0.0)
    unit_lower = lower + jnp.eye(chunk, dtype=lower.dtype)
    u = lax.linalg.triangular_solve(unit_lower, v_beta, left_side=True, lower=True)
    w = lax.linalg.triangular_solve(unit_lower, k_beta * jnp.exp(gcum)[..., None],
                                    left_side=True, lower=True)
    attn = jnp.where(causal, jnp.einsum('bnhik,bnhjk->bnhij', qc, kc) * decay, 0.0)
    q_dec = qc * jnp.exp(gcum)[..., None]
    k_dec = kc * jnp.exp(gcum[..., -1:] - gcum)[..., None]
    blk_dec = jnp.exp(gcum[..., -1])

    def step(s, inp):
        u_i, w_i, a_i, qd_i, kd_i, bd_i = inp
        v_new = u_i - jnp.einsum('bhck,bhkv->bhcv', w_i, s)
        o_i = jnp.einsum('bhck,bhkv->bhcv', qd_i, s) + jnp.einsum('bhcj,bhjv->bhcv', a_i, v_new)
        s = s * bd_i[..., None, None] + jnp.einsum('bhck,bhcv->bhkv', kd_i, v_new)
        return s, o_i

    xs = tuple(jnp.moveaxis(a, 1, 0) for a in (u, w, attn, q_dec, k_dec, blk_dec))
    s_final, o = lax.scan(step, s0, xs)
    o = jnp.transpose(o, (1, 0, 3, 2, 4)).reshape(bsz, t, h, dv)
    return o, s_final


def ssd_chunked(xdt, adt, bm, cm, s0, chunk):
    bsz, t, h, p = xdt.shape
    nst = bm.shape[-1]
    nc = t // chunk
    xc = xdt.reshape(bsz, nc, chunk, h, p)
    bc = bm.reshape(bsz, nc, chunk, h, nst)
    cc = cm.reshape(bsz, nc, chunk, h, nst)
    ac = jnp.moveaxis(adt.reshape(bsz, nc, chunk, h), 3, 2)
    acum = jnp.cumsum(ac, axis=-1)
    causal = jnp.tril(jnp.ones((chunk, chunk), dtype=bool))
    seg = acum[..., :, None] - acum[..., None, :]
    lmat = jnp.where(causal, jnp.exp(jnp.where(causal, seg, 0.0)), 0.0)
    scores = jnp.einsum('bclhn,bcshn->bchls', cc, bc) * lmat
    y_diag = jnp.einsum('bchls,bcshp->bclhp', scores, xc)
    decay_to_end = jnp.exp(acum[..., -1:] - acum)
    chunk_states = jnp.einsum('bclhn,bchl,bclhp->bchpn', bc, decay_to_end, xc)
    chunk_decay = jnp.exp(acum[..., -1])

    def step(s, inp):
        cs, cd = inp
        return s * cd[..., None, None] + cs, s

    s_final, s_in = lax.scan(step, s0, (jnp.moveaxis(chunk_states, 1, 0),
                                        jnp.moveaxis(chunk_decay, 1, 0)))
    s_in = jnp.moveaxis(s_in, 0, 1)
    y_off = jnp.einsum('bclhn,bchpn,bchl->bclhp', cc, s_in, jnp.exp(acum))
    return (y_diag + y_off).reshape(bsz, t, h, p), s_final


def rwkv7_scan(r, w, k, v, a, b, s0):
    def step(s, inp):
        r_t, w_t, k_t, v_t, a_t, b_t = inp
        sa = jnp.einsum('bhvk,bhk->bhv', s, a_t)
        s = s * w_t[:, :, None, :] + sa[..., None] * b_t[:, :, None, :] + v_t[..., None] * k_t[:, :, None, :]
        return s, jnp.einsum('bhvk,bhk->bhv', s, r_t)

    xs = tuple(jnp.moveaxis(z, 1, 0) for z in (r, w, k, v, a, b))
    s_final, ys = lax.scan(step, s0, xs)
    return jnp.moveaxis(ys, 0, 1), s_final


def gdn_mixer(p, s0, conv_prev, lp, chunk):
    bsz, t, _ = p.shape
    h, d = GDN_HEADS, HEAD_DIM
    c0 = GDN_CONV_CH
    c1 = c0 + GDN_WIDTH
    c2 = c1 + GDN_HEADS
    qkv, conv_new = causal_conv(p[..., :c0], conv_prev, lp['gdn_conv_w'])
    qkv = jax.nn.silu(qkv).reshape(bsz, t, 3, h, d)
    q = l2_normalize(qkv[:, :, 0]) * (d ** -0.5)
    k = l2_normalize(qkv[:, :, 1])
    v = qkv[:, :, 2]
    z = p[..., c0:c1].reshape(bsz, t, h, d)
    g = -jnp.exp(lp['gdn_A_log'].astype(jnp.float32)) * jax.nn.softplus(p[..., c1:c2] + lp['gdn_dt_bias'])
    beta = jax.nn.sigmoid(p[..., c2:c2 + h])
    o, s = gated_delta_chunked(q, k, v, g, beta, s0.astype(jnp.float32), chunk)
    o = rms_norm(o, lp['gdn_norm_w']) * jax.nn.silu(z)
    return o.reshape(bsz, t, GDN_WIDTH), s, conv_new


def ssd_mixer(p, s0, conv_prev, lp, chunk):
    bsz, t, _ = p.shape
    h, d, g, n = SSM_HEADS, HEAD_DIM, SSM_GROUPS, SSM_STATE
    c0 = SSM_WIDTH
    c1 = c0 + SSM_CONV_CH
    z = p[..., :c0]
    xbc, conv_new = causal_conv(p[..., c0:c1], conv_prev, lp['ssm_conv_w'])
    xbc = jax.nn.silu(xbc + lp['ssm_conv_b'])
    xs = xbc[..., :SSM_WIDTH].reshape(bsz, t, h, d)
    bm = jnp.repeat(xbc[..., SSM_WIDTH:SSM_WIDTH + SSM_BC].reshape(bsz, t, g, n), h // g, axis=2)
    cm = jnp.repeat(xbc[..., SSM_WIDTH + SSM_BC:].reshape(bsz, t, g, n), h // g, axis=2)
    dt = jax.nn.softplus(p[..., c1:c1 + h] + lp['ssm_dt_bias'])
    a = -jnp.exp(lp['ssm_A_log'].astype(jnp.float32))
    y, s = ssd_chunked(xs * dt[..., None], dt * a, bm, cm, s0.astype(jnp.float32), chunk)
    y = y + lp['ssm_D'].astype(jnp.float32)[:, None] * xs
    y = y.reshape(bsz, t, SSM_WIDTH) * jax.nn.silu(z)
    yg = y.reshape(bsz, t, g, SSM_WIDTH // g)
    yg = yg * lax.rsqrt(jnp.mean(yg * yg, axis=-1, keepdims=True) + NORM_EPS)
    y = yg.reshape(bsz, t, SSM_WIDTH) * lp['ssm_norm_w'].astype(jnp.float32)
    return y, s, conv_new


def rwkv7_mixer(p, s0, shift_prev, lp):
    bsz, t, _ = p.shape
    h, d, wd = RWKV_HEADS, HEAD_DIM, RWKV_WIDTH
    prev = jnp.concatenate([shift_prev.astype(p.dtype), p[:, :-1]], axis=1)
    xm = p + (prev - p) * lp['rwkv_mu']
    c0 = 3 * wd
    c1 = c0 + RWKV_DECAY_LORA
    c2 = c1 + RWKV_ICLR_LORA
    r = xm[..., :wd]
    k = xm[..., wd:2 * wd]
    v = xm[..., 2 * wd:c0]
    w_log = -jax.nn.softplus(-(lp['rwkv_w0'] + jnp.matmul(jnp.tanh(xm[..., c0:c1]), lp['rwkv_w_up']))) - 0.5
    decay = jnp.exp(-jnp.exp(w_log))
    iclr = jax.nn.sigmoid(lp['rwkv_a0'] + jnp.matmul(xm[..., c1:c2], lp['rwkv_a_up']))
    gate = jnp.matmul(jax.nn.sigmoid(xm[..., c2:]), lp['rwkv_g_up'])

    def heads(z):
        return z.reshape(bsz, t, h, d)

    kk = l2_normalize(heads(k * lp['rwkv_k_k']))
    k = k * (1.0 + (iclr - 1.0) * lp['rwkv_k_a'])
    r_h, k_h, v_h = heads(r), heads(k), heads(v)
    y, s = rwkv7_scan(r_h, heads(decay), k_h, v_h, -kk, kk * heads(iclr), s0.astype(jnp.float32))
    mean = jnp.mean(y, axis=-1, keepdims=True)
    var = jnp.mean(jnp.square(y - mean), axis=-1, keepdims=True)
    y = ((y - mean) * lax.rsqrt(var + RWKV_GN_EPS)).reshape(bsz, t, wd)
    y = y * lp['rwkv_ln_w'] + lp['rwkv_ln_b']
    bonus = jnp.sum(r_h * k_h * lp['rwkv_r_k'], axis=-1, keepdims=True) * v_h
    y = (y + bonus.reshape(bsz, t, wd)) * gate
    return y, s, p[:, -1:]


def layer(x, st, lp):
    gdn_s, gdn_conv, ssm_s, ssm_conv, rwkv_s, rwkv_shift = st
    chunk = min(CHUNK, x.shape[1])
    hn = rms_norm(x, lp['norm1_w'])
    proj = jnp.matmul(hn, lp['w_in']).astype(jnp.float32)
    pa = proj[..., :GDN_COLS]
    pb = proj[..., GDN_COLS:GDN_COLS + SSM_COLS]
    pc = proj[..., GDN_COLS + SSM_COLS:]
    ya, gdn_s_new, gdn_conv_new = gdn_mixer(pa, gdn_s, gdn_conv, lp, chunk)
    yb, ssm_s_new, ssm_conv_new = ssd_mixer(pb, ssm_s, ssm_conv, lp, chunk)
    yc, rwkv_s_new, rwkv_shift_new = rwkv7_mixer(pc, rwkv_s, rwkv_shift, lp)
    mix = jnp.concatenate([ya, yb, yc], axis=-1).astype(x.dtype)
    x = x + jnp.matmul(mix, lp['w_out'])
    h2 = rms_norm(x, lp['norm2_w'])
    ff = jax.nn.silu(jnp.matmul(h2, lp['ffn_w_gate'])) * jnp.matmul(h2, lp['ffn_w_up'])
    x = x + jnp.matmul(ff, lp['ffn_w_down'])
    return x, (gdn_s_new, gdn_conv_new, ssm_s_new, ssm_conv_new, rwkv_s_new, rwkv_shift_new)


def trunk(x, states, params, final_norm_w):
    outs = [[] for _ in states]
    for l in range(DEPTH):
        lp = {name: w[l] for name, w in params.items()}
        x, new = layer(x, tuple(s[l] for s in states), lp)
        for o, nw in zip(outs, new):
            o.append(nw.astype(x.dtype))
    y = rms_norm(x, final_norm_w)
    return y, tuple(jnp.stack(o) for o in outs)


def setup_inputs(seed: int = 0) -> dict:
    key = jax.random.key(seed)
    ks = iter(jax.random.split(key, 48))

    def nrm(shape, scale):
        return scale * jax.random.normal(next(ks), shape, jnp.float32)

    def unif(shape, lo, hi):
        return jax.random.uniform(next(ks), shape, jnp.float32, lo, hi)

    def dt_bias(shape):
        dt = jnp.exp(unif(shape, math.log(1e-3), math.log(1e-1)))
        return dt + jnp.log(-jnp.expm1(-dt))

    L, D, W1 = DEPTH, D_MODEL, CONV_WIDTH - 1
    return {
        'x_prompt': nrm((BATCH, SEQ, D), 1.0),
        'x_sample': nrm((DEC_BATCH, DEC_SEQ, D), 1.0),
        'state_gdn': nrm((L, DEC_BATCH, GDN_HEADS, HEAD_DIM, HEAD_DIM), 0.3),
        'state_gdn_conv': nrm((L, DEC_BATCH, W1, GDN_CONV_CH), 1.0),
        'state_ssm': nrm((L, DEC_BATCH, SSM_HEADS, HEAD_DIM, SSM_STATE), 0.3),
        'state_ssm_conv': nrm((L, DEC_BATCH, W1, SSM_CONV_CH), 1.0),
        'state_rwkv': nrm((L, DEC_BATCH, RWKV_HEADS, HEAD_DIM, HEAD_DIM), 0.3),
        'state_rwkv_shift': nrm((L, DEC_BATCH, 1, RWKV_COLS), 1.0),
        'norm1_w': 1.0 + nrm((L, D), 0.02),
        'w_in': nrm((L, D, IN_COLS), D ** -0.5),
        'gdn_conv_w': nrm((L, CONV_WIDTH, GDN_CONV_CH), CONV_WIDTH ** -0.5),
        'gdn_A_log': jnp.log(unif((L, GDN_HEADS), 1.0, 16.0)),
        'gdn_dt_bias': dt_bias((L, GDN_HEADS)),
        'gdn_norm_w': 1.0 + nrm((L, HEAD_DIM), 0.02),
        'ssm_conv_w': nrm((L, CONV_WIDTH, SSM_CONV_CH), CONV_WIDTH ** -0.5),
        'ssm_conv_b': nrm((L, SSM_CONV_CH), 0.02),
        'ssm_A_log': jnp.log(unif((L, SSM_HEADS), 1.0, 16.0)),
        'ssm_dt_bias': dt_bias((L, SSM_HEADS)),
        'ssm_D': 1.0 + nrm((L, SSM_HEADS), 0.1),
        'ssm_norm_w': 1.0 + nrm((L, SSM_WIDTH), 0.02),
        'rwkv_mu': unif((L, RWKV_COLS), 0.0, 1.0),
        'rwkv_w0': unif((L, RWKV_WIDTH), -6.5, -1.5),
        'rwkv_w_up': nrm((L, RWKV_DECAY_LORA, RWKV_WIDTH), 0.5 * RWKV_DECAY_LORA ** -0.5),
        'rwkv_a0': nrm((L, RWKV_WIDTH), 0.1),
        'rwkv_a_up': nrm((L, RWKV_ICLR_LORA, RWKV_WIDTH), 0.5 * RWKV_ICLR_LORA ** -0.5),
        'rwkv_g_up': nrm((L, RWKV_GATE_LORA, RWKV_WIDTH), RWKV_GATE_LORA ** -0.5),
        'rwkv_k_k': 0.85 + nrm((L, RWKV_WIDTH), 0.02),
        'rwkv_k_a': 1.0 + nrm((L, RWKV_WIDTH), 0.02),
        'rwkv_r_k': nrm((L, RWKV_HEADS, HEAD_DIM), 0.1),
        'rwkv_ln_w': 1.0 + nrm((L, RWKV_WIDTH), 0.02),
        'rwkv_ln_b': nrm((L, RWKV_WIDTH), 0.02),
        'w_out': nrm((L, MIX_WIDTH, D), MIX_WIDTH ** -0.5),
        'norm2_w': 1.0 + nrm((L, D), 0.02),
        'ffn_w_gate': nrm((L, D, FFN_HIDDEN), D ** -0.5),
        'ffn_w_up': nrm((L, D, FFN_HIDDEN), D ** -0.5),
        'ffn_w_down': nrm((L, FFN_HIDDEN, D), FFN_HIDDEN ** -0.5),
        'final_norm_w': 1.0 + nrm((D,), 0.02),
    }


def reference(x_prompt, x_sample, state_gdn, state_gdn_conv, state_ssm, state_ssm_conv,
              state_rwkv, state_rwkv_shift,
              norm1_w, w_in, gdn_conv_w, gdn_A_log, gdn_dt_bias, gdn_norm_w,
              ssm_conv_w, ssm_conv_b, ssm_A_log, ssm_dt_bias, ssm_D, ssm_norm_w,
              rwkv_mu, rwkv_w0, rwkv_w_up, rwkv_a0, rwkv_a_up, rwkv_g_up,
              rwkv_k_k, rwkv_k_a, rwkv_r_k, rwkv_ln_w, rwkv_ln_b,
              w_out, norm2_w, ffn_w_gate, ffn_w_up, ffn_w_down, final_norm_w):
    params = {
        'norm1_w': norm1_w, 'w_in': w_in,
        'gdn_conv_w': gdn_conv_w, 'gdn_A_log': gdn_A_log, 'gdn_dt_bias': gdn_dt_bias,
        'gdn_norm_w': gdn_norm_w,
        'ssm_conv_w': ssm_conv_w, 'ssm_conv_b': ssm_conv_b, 'ssm_A_log': ssm_A_log,
        'ssm_dt_bias': ssm_dt_bias, 'ssm_D': ssm_D, 'ssm_norm_w': ssm_norm_w,
        'rwkv_mu': rwkv_mu, 'rwkv_w0': rwkv_w0, 'rwkv_w_up': rwkv_w_up, 'rwkv_a0': rwkv_a0,
        'rwkv_a_up': rwkv_a_up, 'rwkv_g_up': rwkv_g_up, 'rwkv_k_k': rwkv_k_k,
        'rwkv_k_a': rwkv_k_a, 'rwkv_r_k': rwkv_r_k, 'rwkv_ln_w': rwkv_ln_w, 'rwkv_ln_b': rwkv_ln_b,
        'w_out': w_out, 'norm2_w': norm2_w,
        'ffn_w_gate': ffn_w_gate, 'ffn_w_up': ffn_w_up, 'ffn_w_down': ffn_w_down,
    }
    nb = x_prompt.shape[0]
    f32 = jnp.float32
    zero_states = (
        jnp.zeros((DEPTH, nb, GDN_HEADS, HEAD_DIM, HEAD_DIM), f32),
        jnp.zeros((DEPTH, nb, CONV_WIDTH - 1, GDN_CONV_CH), f32),
        jnp.zeros((DEPTH, nb, SSM_HEADS, HEAD_DIM, SSM_STATE), f32),
        jnp.zeros((DEPTH, nb, CONV_WIDTH - 1, SSM_CONV_CH), f32),
        jnp.zeros((DEPTH, nb, RWKV_HEADS, HEAD_DIM, HEAD_DIM), f32),
        jnp.zeros((DEPTH, nb, 1, RWKV_COLS), f32),
    )
    y_prompt, p_states = trunk(x_prompt, zero_states, params, final_norm_w)
    p_gdn, p_gdn_conv, p_ssm, p_ssm_conv, p_rwkv, p_rwkv_shift = p_states
    cache_states = (state_gdn, state_gdn_conv, state_ssm, state_ssm_conv, state_rwkv, state_rwkv_shift)
    y_sample, s_states = trunk(x_sample, cache_states, params, final_norm_w)
    s_gdn, s_gdn_conv, s_ssm, s_ssm_conv, s_rwkv, s_rwkv_shift = s_states
    return (y_prompt, y_sample,
            p_gdn, p_gdn_conv, p_ssm, p_ssm_conv, p_rwkv, p_rwkv_shift,
            s_gdn, s_gdn_conv, s_ssm, s_ssm_conv, s_rwkv, s_rwkv_shift)
```

```python
import numpy as np
import concourse.bass as bass
import concourse.mybir as mybir
from concourse.bass_utils import run_bass_kernel_spmd
from contextlib import ExitStack

F32 = mybir.dt.float32
BF16 = mybir.dt.bfloat16
AF = mybir.ActivationFunctionType
ALU = mybir.AluOpType
AX = mybir.AxisListType

D = 1024
INC = 3600
FH = 2816
NJ = 22
GQ, GK, GV, GZ, GA, GB_ = 0, 256, 512, 768, 1024, 1028
SZ, SX, SBc, SCc, SDT = 1032, 1544, 2056, 2312, 2568
RR = 2576
NSEQ = 4
import os
USE_RR = bool(os.environ.get('USE_RR'))
SHARD = bool(int(os.environ.get('KSHARD', '0')))
ROT = 12
EPS = 1e-6
GN_EPS = 64e-5


class Sem:
    def __init__(self, h, name):
        self.h = h
        self.name = name
        self.count = 0


class Buf:
    def __init__(self, t, name):
        self.t = t
        self.name = name
        self.lw = None
        self.rd = []

    def __getitem__(self, k):
        return self.t[k]


class Em:
    def __init__(self, nc, es):
        self.nc = nc
        self.es = es
        self.eng = {'pe': nc.tensor, 'act': nc.scalar, 'dve': nc.vector, 'pool': nc.gpsimd, 'sp': nc.sync}
        self.sems = []
        self.esem = {k: self.newsem('e_' + k) for k in ['pe', 'act', 'dve', 'pool']}
        self.known = {k: {} for k in self.eng}
        self.n_ins = 0
        self.n_wait = 0

    def newsem(self, name):
        s = Sem(self.es.enter_context(self.nc.semaphore(name)), name)
        self.sems.append(s)
        return s

    def sb(self, name, shape, dt=F32, es=None):
        self.n_sb = getattr(self, 'n_sb', 0) + 1
        name = "%s_%d" % (name, self.n_sb)
        if es is not None:
            if not hasattr(self, 'scope_bufs'):
                self.scope_bufs = {}
            b = Buf(es.enter_context(self.nc.sbuf_tensor(name, shape, dt)), name)
            self.scope_bufs.setdefault(id(es), []).append(b)
            return b
        return Buf((es or self.es).enter_context(self.nc.sbuf_tensor(name, shape, dt)), name)

    def ps(self, name, shape, dt=F32):
        return Buf(self.es.enter_context(self.nc.psum_tensor(name, shape, dt)), name)

    def dram(self, name, shape, dt=F32, kind=None):
        if kind is None:
            t = self.nc.dram_tensor(name, shape, dt)
        else:
            t = self.nc.dram_tensor(name, shape, dt, kind=kind)
        return Buf(t, name)

    def _waits(self, e, reads, writes):
        deps = {}

        def add(ent, raw):
            if ent is None:
                return
            s, v, en = ent
            if en == e and (e == 'pe' or not raw):
                return
            if deps.get(s, 0) < v:
                deps[s] = v
        for b in reads:
            add(b.lw, True)
        for b in writes:
            add(b.lw, False)
            for r in b.rd:
                add(r, False)
        kn = self.known[e]
        for s, v in deps.items():
            if kn.get(s, 0) >= v:
                continue
            self.eng[e].wait_ge(s.h, v)
            kn[s] = v
            self.n_wait += 1

    def op(self, e, fn, reads=(), writes=()):
        self._waits(e, reads, writes)
        ins = fn(self.eng[e])
        s = self.esem[e]
        s.count += 1
        ins.then_inc(s.h, 1)
        ent = (s, s.count, e)
        for b in reads:
            b.rd.append(ent)
            if len(b.rd) > 24:
                b.rd = b.rd[-24:] if False else b.rd
        for b in writes:
            b.lw = ent
            b.rd = []
        self.n_ins += 1
        return ins

    def buf_sem(self, b):
        if getattr(b, 'dsem', None) is None:
            if not hasattr(self, 'free_dsems'):
                self.free_dsems = []
                self.n_dsem = 0
            if self.free_dsems:
                b.dsem = self.free_dsems.pop()
            else:
                self.n_dsem += 1
                b.dsem = self.newsem("d%d" % self.n_dsem)
        return b.dsem

    def end_phase(self, bufs):
        self.barrier()
        for b in bufs:
            if getattr(b, 'dsem', None) is not None and hasattr(self, 'free_dsems'):
                self.free_dsems.append(b.dsem)
                b.dsem = None

    def coll(self, sem, reads, writes, fn):
        self._waits('pool', reads, writes)
        ins = fn(self.eng['pool'])
        sem.count += 1
        ins.then_inc(sem.h, 1)
        ent = (sem, sem.count, 'dma')
        for b in reads:
            b.rd.append(ent)
        for b in writes:
            b.lw = ent
            b.rd = []
        self.n_ins += 1
        return ins

    def dma(self, q, sem, out_ap, in_ap, reads=(), writes=()):
        self._waits(q, reads, writes)
        if not USE_RR:
            sem = self.buf_sem(writes[0])
        ins = self.eng[q].dma_start(out=out_ap, in_=in_ap)
        sem.count += 16
        ins.then_inc(sem.h, 16)
        ent = (sem, sem.count, 'dma')
        for b in reads:
            b.rd.append(ent)
        for b in writes:
            b.lw = ent
            b.rd = []
        self.n_ins += 1
        return ins

    def rotate(self):
        self.barrier()
        for k in list(self.esem):
            self.n_rot = getattr(self, 'n_rot', 0) + 1
            if not hasattr(self, 'rot_pool'):
                self.rot_pool = {}
            pool = self.rot_pool.setdefault(k, [])
            if len(pool) < 2:
                pool.append(self.esem[k])
                if len(pool) < 2:
                    self.esem[k] = self.newsem('e2_' + k)
                    continue
            cur = self.esem[k]
            other = pool[0] if pool[1] is cur else pool[1]
            if cur not in pool:
                pool.append(cur)
            self.esem[k] = other

    def barrier(self):
        for e in self.eng:
            kn = self.known[e]
            for s in self.sems:
                if s.count > 0 and kn.get(s, 0) < s.count:
                    self.eng[e].wait_ge(s.h, s.count)
                    kn[s] = s.count
                    self.n_wait += 1


def host_consts():
    p = np.arange(128)[:, None]
    i = np.arange(128)[None, :]
    same = (p // 64) == (i // 64)
    c = {}
    c['tri_incl'] = (same & (p <= i)).astype(np.float32)
    c['tri_le'] = (p <= i).astype(np.float32)
    c['sups'] = (p > i).astype(np.float32)
    c['mstrict'] = (same & (i > p)).astype(np.float32)
    c['blk'] = same.astype(np.float32)
    c['ident'] = np.eye(128, dtype=np.float32)
    rowc = np.zeros((128, 2, 128), np.float32)
    rowc[0:64, 0, :] = 1
    rowc[64:128, 1, :] = 1
    c['rowc'] = rowc.reshape(128, 256)
    ch = np.zeros((128, 2), np.float32)
    ch[0:64, 0] = 1
    ch[64:, 1] = 1
    c['chsel'] = ch
    return np.concatenate([c[k] for k in ['tri_incl', 'tri_le', 'sups', 'mstrict', 'blk', 'ident', 'rowc', 'chsel']], axis=1)


NCONST = 128 * 6 + 256 + 2


def build(PTOK, shard=SHARD):
    assert PTOK % 128 == 0
    PT = PTOK // 128
    NT = PT + 2
    NTOK = NT * 128
    SROWS = 3 + PTOK + NSEQ * 67
    nc = bass.Bass("TRN2", target_bir_lowering=False)
    es = ExitStack()
    with es:
        em = Em(nc, es)
        DI = lambda n, s: em.dram(n, s, F32, kind="ExternalInput")
        DO = lambda n, s: em.dram(n, s, F32, kind="ExternalOutput")
        xin = DI("xin", [NTOK, D])
        xhalo = DI("xhalo", [128, D])
        oh_me = DI("oh_me", [128, 8])
        selprev = DI("selprev", [24, 128])
        consts_d = DI("consts", [128, NCONST])
        st_gdn = DI("st_gdn", [2, NSEQ, 4, 64, 64])
        st_gdn_conv = DI("st_gdn_conv", [2, NSEQ, 3, 768])
        st_ssm = DI("st_ssm", [2, NSEQ, 8, 64, 128])
        st_ssm_conv = DI("st_ssm_conv", [2, NSEQ, 3, 1024])
        st_rwkv = DI("st_rwkv", [2, NSEQ, 4, 64, 64])
        st_rwkv_shift = DI("st_rwkv_shift", [2, NSEQ, 1, 1024])
        norm1_w = DI("norm1_w", [2, 128, 8]); w_in = DI("w_in", [2, D, INC])
        convw = DI("convw", [2, 4, 1792]); convb = DI("convb", [2, 1792])
        gdn_A_log = DI("gdn_A_log", [2, 4]); gdn_dt_bias = DI("gdn_dt_bias", [2, 4]); gdn_norm_w = DI("gdn_norm_w", [2, 256])
        ssm_A_log = DI("ssm_A_log", [2, 8]); ssm_dt_bias = DI("ssm_dt_bias", [2, 8]); ssm_D = DI("ssm_D", [2, 8])
        ssm_norm_w = DI("ssm_norm_w", [2, 512])
        rwkv_mu = DI("rwkv_mu", [2, 1024])
        rvec = DI("rvec", [2, 7, 256])
        rwkv_w_up = DI("rwkv_w_up", [2, 64, 256]); rwkv_a_up = DI("rwkv_a_up", [2, 64, 256]); rwkv_g_up = DI("rwkv_g_up", [2, 128, 256])
        w_out = DI("w_out", [2, D, D]); norm2_w = DI("norm2_w", [2, 128, 8])
        w_gate = DI("ffn_w_gate", [2, D, FH]); w_up = DI("ffn_w_up", [2, D, FH]); w_down = DI("ffn_w_down", [2, FH, D])
        final_norm_w = DI("final_norm_w", [1, D])

        y_out = DO("y", [NTOK, D])
        o_pgdn = DO("o_pgdn", [2, 4, 64, 64]); o_pgdn_conv = DO("o_pgdn_conv", [2, 3, 768])
        o_pssm = DO("o_pssm", [2, 8, 64, 128]); o_pssm_conv = DO("o_pssm_conv", [2, 3, 1024])
        o_prwkv = DO("o_prwkv", [2, 4, 64, 64]); o_prwkv_shift = DO("o_prwkv_shift", [2, 1, 1024])
        o_sgdn = DO("o_sgdn", [2, NSEQ, 4, 64, 64]); o_sgdn_conv = DO("o_sgdn_conv", [2, NSEQ, 3, 768])
        o_sssm = DO("o_sssm", [2, NSEQ, 8, 64, 128]); o_sssm_conv = DO("o_sssm_conv", [2, NSEQ, 3, 1024])
        o_srwkv = DO("o_srwkv", [2, NSEQ, 4, 64, 64]); o_srwkv_shift = DO("o_srwkv_shift", [2, NSEQ, 1, 1024])
        outs_all = [y_out, o_pgdn, o_pgdn_conv, o_pssm, o_pssm_conv, o_prwkv, o_prwkv_shift,
                    o_sgdn, o_sgdn_conv, o_sssm, o_sssm_conv, o_srwkv, o_srwkv_shift]

        win_b = em.dram("win_b", [2, D, INC], BF16)
        wout_b = em.dram("wout_b", [2, D, D], BF16)
        wg_b = em.dram("wg_b", [2, D, FH], BF16)
        wu_b = em.dram("wu_b", [2, D, FH], BF16)
        wd_b = em.dram("wd_b", [2, FH, D], BF16)
        projs = em.dram("projs", [SROWS, INC], F32)
        mixTs = em.dram("mixTs", [NT, 128, 8, 128], BF16)
        x1s = em.dram("x1s", [NTOK, D], F32)
        AGW = 1544
        agin = em.dram("agin", [128, AGW], F32)
        agout = em.dram("agout", [8 * 128, AGW], F32)
        hin = em.dram("hin", [3, D], F32)
        hout = em.dram("hout", [24, D], F32)
        ccs = em.newsem("ccs")
        ccs_l = [em.newsem("ccs_l%d" % i) for i in range(2)]
        gSin = em.sb("gSin", [64, 256]); rSin = em.sb("rSin", [64, 256]); sSin = em.sb("sSin", [128, 512])

        PS = [em.ps("ps%d" % i, [128, 512]) for i in range(8)]
        psi = [0]

        def P():
            b = PS[psi[0] % 8]
            psi[0] += 1
            return b

        ld = [em.newsem("ld%d" % i) for i in range(4)]
        ldi = [0]

        def LD(out_ap, in_ap, r, w):
            s = ld[ldi[0] % 4]
            ldi[0] += 1
            em.dma('sp', s, out_ap, in_ap, reads=r, writes=w)

        stq = em.newsem("stq")

        def ST(out_ap, in_ap, r, w):
            em.dma('sp', stq, out_ap, in_ap, reads=r, writes=w)

        def mm(out_ap, lhsT, rhs, r, w, start=True, stop=True):
            em.op('pe', lambda e: e.matmul(out_ap, lhsT=lhsT, rhs=rhs, start=start, stop=stop), r, w)

        def act(out_ap, in_ap, func, r, w, bias=None, scale=None, accum=None):
            kw = {}
            if bias is not None:
                kw['bias'] = bias
            if scale is not None:
                kw['scale'] = scale
            if accum is not None:
                kw['accum_out'] = accum
            em.op('act', lambda e: e.activation(out=out_ap, in_=in_ap, func=func, **kw), r, w)

        def tt(eng, out_ap, in0, in1, op, r, w):
            em.op(eng, lambda e: e.tensor_tensor(out=out_ap, in0=in0, in1=in1, op=op), r, w)

        def tsc(eng, out_ap, in0, s1, op0, r, w, s2=None, op1=None):
            if op1 is None:
                em.op(eng, lambda e: e.tensor_scalar(out=out_ap, in0=in0, scalar1=s1, scalar2=None, op0=op0), r, w)
            else:
                em.op(eng, lambda e: e.tensor_scalar(out=out_ap, in0=in0, scalar1=s1, scalar2=s2, op0=op0, op1=op1), r, w)

        def stt(eng, out_ap, in0, scalar, in1, op0, op1, r, w):
            em.op(eng, lambda e: e.scalar_tensor_tensor(out=out_ap, in0=in0, scalar=scalar, in1=in1, op0=op0, op1=op1), r, w)

        def red(out_ap, in_ap, r, w):
            em.op('dve', lambda e: e.tensor_reduce(out=out_ap, in_=in_ap, axis=AX.X, op=ALU.add), r, w)

        def sigmoid(out_ap, in_ap, r, w, scale=1.0):
            act(out_ap, in_ap, AF.Exp, r, w, scale=-scale)
            act(out_ap, out_ap, AF.Ln, w, w, bias=1.0)
            act(out_ap, out_ap, AF.Exp, w, w, scale=-1.0)

        def rsqrt(out_ap, in_ap, r, w, eps_ap, scale=1.0):
            act(out_ap, in_ap, AF.Ln, r + [epsb], w, bias=eps_ap, scale=scale)
            act(out_ap, out_ap, AF.Exp, w, w, scale=-0.5)

        cst = em.sb("cst", [128, NCONST])
        LD(cst[:], consts_d[:], [consts_d], [cst])
        TRI = cst[:, 0:128]; TLE = cst[:, 128:256]; SUPS = cst[:, 256:384]; MSTR = cst[:, 384:512]
        BLK = cst[:, 512:640]; IDF = cst[:, 640:768]
        ROWC = lambda c: cst[:, 768 + c * 128: 768 + (c + 1) * 128]
        CHSEL = cst[:, 1024:1026]
        identb = em.sb("identb", [128, 128], BF16)
        act(identb[:], IDF, AF.Copy, [cst], [identb])
        epsb = em.sb("epsb", [128, 2])
        em.op('pool', lambda e: e.memset(epsb[:, 0:1], EPS), [], [epsb])
        em.op('pool', lambda e: e.memset(epsb[:, 1:2], GN_EPS), [epsb], [epsb])
        zero_t = em.sb("zero_t", [3, INC])
        em.op('pool', lambda e: e.memset(zero_t[:], 0.0), [], [zero_t])

        with ExitStack() as wes:
            stg = [em.sb("stg%d" % i, [128, INC], F32, wes) for i in range(2)]
            stb = [em.sb("stb%d" % i, [128, INC], BF16, wes) for i in range(2)]
            k = 0
            for l in range(2):
                for (src, dst, rows, cols) in [(w_in, win_b, D, INC), (w_out, wout_b, D, D), (w_gate, wg_b, D, FH),
                                               (w_up, wu_b, D, FH), (w_down, wd_b, FH, D)]:
                    for r0 in range(0, rows, 128):
                        a, b = stg[k % 2], stb[k % 2]
                        LD(a[:, 0:cols], src[l, r0:r0 + 128, :], [src], [a])
                        if k % 3 == 0:
                            act(b[:, 0:cols], a[:, 0:cols], AF.Copy, [a], [b])
                        elif k % 3 == 1:
                            em.op('dve', lambda e: e.tensor_copy(out=b[:, 0:cols], in_=a[:, 0:cols]), [a], [b])
                        else:
                            em.op('pool', lambda e: e.tensor_copy(out=b[:, 0:cols], in_=a[:, 0:cols]), [a], [b])
                        ST(dst[l, r0:r0 + 128, :], b[:, 0:cols], [b], [dst])
                        k += 1
            em.end_phase(em.scope_bufs.get(id(wes), []))

        def seq_rows(t, c):
            if t < PT:
                return 3 + t * 128 + c * 64
            s = 2 * (t - PT) + c
            return 3 + PTOK + s * 67 + 3

        for l in range(2):
            xsrc = xin if l == 0 else x1s
            for s in range(NSEQ):
                b0 = 3 + PTOK + s * 67
                ST(projs[b0:b0 + 3, 0:768], st_gdn_conv[l, s], [st_gdn_conv], [projs])
                ST(projs[b0:b0 + 3, SX:SX + 1024], st_ssm_conv[l, s], [st_ssm_conv], [projs])
                ST(projs[b0 + 2:b0 + 3, RR:RR + 1024], st_rwkv_shift[l, s], [st_rwkv_shift], [projs])

            with ExitStack() as pes:
                winr = em.sb("winr", [128, 8, INC], BF16, pes)
                for kk in range(8):
                    LD(winr[:, kk, :], win_b[l, kk * 128:(kk + 1) * 128, :], [win_b], [winr])
                n1w = em.sb("n1w", [128, 8], F32, pes)
                LD(n1w[:], norm1_w[l], [norm1_w], [n1w])
                xt = [em.sb("xt%d" % i, [128, D], F32, pes) for i in range(2)]
                junk = em.sb("junk", [128, D], F32, pes)
                ssq = [em.sb("ssq%d" % i, [128, 2], F32, pes) for i in range(2)]
                hnb = [em.sb("hnb%d" % i, [128, D], BF16, pes) for i in range(2)]
                hnT = [em.sb("hnT%d" % i, [128, 8, 128], BF16, pes) for i in range(2)]
                pj = [em.sb("pj%d" % i, [128, INC], F32, pes) for i in range(2)]
                if l == 1 and shard:
                    hall = em.sb("hall", [24, D], F32, pes)
                    selp = em.sb("selp", [24, 128], F32, pes)
                    LD(hall[:], hout[:], [hout], [hall])
                    LD(selp[:], selprev[:], [selprev], [selp])
                for t in range(-1, NT):
                    q = t % 2
                    if t >= 0:
                        LD(xt[q][:], xsrc[t * 128:(t + 1) * 128, :], [xsrc], [xt[q]])
                    elif l == 0 or not shard:
                        LD(xt[q][:], xhalo[:], [xhalo], [xt[q]])
                    else:
                        for cb_ in range(2):
                            pb = P()
                            mm(pb[:], selp[:], hall[:, cb_ * 512:(cb_ + 1) * 512], [selp, hall], [pb])
                            act(xt[q][:, cb_ * 512:(cb_ + 1) * 512], pb[:], AF.Copy, [pb], [xt[q]])
                    act(junk[:], xt[q][:], AF.Square, [xt[q]], [junk, ssq[q]], accum=ssq[q][:, 0:1])
                    rsqrt(ssq[q][:, 1:2], ssq[q][:, 0:1], [ssq[q]], [ssq[q]], epsb[:, 0:1], scale=1.0 / D)
                    act(hnb[q][:], xt[q][:], AF.Copy, [xt[q], ssq[q]], [hnb[q]], scale=ssq[q][:, 1:2])
                    for half in range(2):
                        pb = P()
                        for kk in range(4):
                            kc = half * 4 + kk
                            mm(pb[:, kk * 128:(kk + 1) * 128], hnb[q][:, kc * 128:(kc + 1) * 128], identb[:],
                               [hnb[q], identb], [pb])
                        for kk in range(4):
                            kc = half * 4 + kk
                            tsc('dve', hnT[q][:, kc, :], pb[:, kk * 128:(kk + 1) * 128], n1w[:, kc:kc + 1], ALU.mult,
                                [pb, n1w], [hnT[q]])
                    for c0 in range(0, INC, 512):
                        cw_ = min(512, INC - c0)
                        pb = P()
                        for kc in range(8):
                            mm(pb[:, 0:cw_], hnT[q][:, kc, :], winr[:, kc, c0:c0 + cw_], [hnT[q], winr], [pb],
                               start=(kc == 0), stop=(kc == 7))
                        act(pj[q][:, c0:c0 + cw_], pb[:, 0:cw_], AF.Copy, [pb], [pj[q]])
                    if t < 0:
                        ST(projs[0:3, :], pj[q][0:3, :], [pj[q]], [projs])
                        continue
                    for c in range(2):
                        r0 = seq_rows(t, c)
                        if t < PT and c == 1:
                            continue
                        nr = 128 if t < PT else 64
                        ST(projs[r0:r0 + nr, :], pj[q][c * 64:c * 64 + nr, :], [pj[q]], [projs])
                em.end_phase(em.scope_bufs.get(id(pes), []))

            e0 = 3 + PTOK
            ST(o_pgdn_conv[l], projs[e0 - 3:e0, 0:768], [projs], [o_pgdn_conv])
            ST(o_pssm_conv[l], projs[e0 - 3:e0, SX:SX + 1024], [projs], [o_pssm_conv])
            ST(o_prwkv_shift[l], projs[e0 - 1:e0, RR:RR + 1024], [projs], [o_prwkv_shift])
            for s in range(NSEQ):
                e1 = 3 + PTOK + s * 67 + 67
                ST(o_sgdn_conv[l, s], projs[e1 - 3:e1, 0:768], [projs], [o_sgdn_conv])
                ST(o_sssm_conv[l, s], projs[e1 - 3:e1, SX:SX + 1024], [projs], [o_sssm_conv])
                ST(o_srwkv_shift[l, s], projs[e1 - 1:e1, RR:RR + 1024], [projs], [o_srwkv_shift])

            def phase_M(mode):
                SO = (mode == "state")
                W = 128 if SO else 64
                with ExitStack() as mes:
                    def bc(name, src_ap, n, dt=F32):
                        b = em.sb(name, [128, n], dt, mes)
                        LD(b[:], src_ap.partition_broadcast(128), [], [b])
                        return b
                    cwt = em.sb("cwt", [128, 4, 1792], F32, mes)
                    for j in range(4):
                        LD(cwt[:, j, :], convw[l, j:j + 1, :].partition_broadcast(128), [], [cwt])
                    cbt = bc("cbt", convb[l:l + 1, :], 1792)
                    mut = bc("mut", rwkv_mu[l:l + 1, :], 1024)
                    rv = em.sb("rv", [128, 7, 256], F32, mes)
                    for j in range(7):
                        LD(rv[:, j, :], rvec[l, j:j + 1, :].partition_broadcast(128), [], [rv])
                    W0, A0, KK_, KA_, RK_, LNW, LNB = [rv[:, j, :] for j in range(7)]
                    snw = bc("snw", ssm_norm_w[l:l + 1, :], 512)
                    gnw = bc("gnw", gdn_norm_w[l:l + 1, :], 256)
                    sm = em.sb("smallc", [128, 64], F32, mes)
                    LD(sm[:, 0:4], gdn_A_log[l:l + 1, :].partition_broadcast(128), [], [sm])
                    LD(sm[:, 4:8], gdn_dt_bias[l:l + 1, :].partition_broadcast(128), [], [sm])
                    LD(sm[:, 8:16], ssm_A_log[l:l + 1, :].partition_broadcast(128), [], [sm])
                    LD(sm[:, 16:24], ssm_dt_bias[l:l + 1, :].partition_broadcast(128), [], [sm])
                    LD(sm[:, 24:32], ssm_D[l:l + 1, :].partition_broadcast(128), [], [sm])
                    act(sm[:, 32:36], sm[:, 0:4], AF.Exp, [sm], [sm])
                    act(sm[:, 40:48], sm[:, 8:16], AF.Exp, [sm], [sm])
                    GAe, GDTB, SAe, SDTB, SDD = sm[:, 32:36], sm[:, 4:8], sm[:, 40:48], sm[:, 16:24], sm[:, 24:32]
                    lf = em.sb("loraf", [128, 3, 256], F32, mes)
                    LD(lf[0:64, 0, :], rwkv_w_up[l], [], [lf])
                    LD(lf[0:64, 1, :], rwkv_a_up[l], [], [lf])
                    LD(lf[:, 2, :], rwkv_g_up[l], [], [lf])
                    lb = em.sb("lorab", [128, 3, 256], BF16, mes)
                    act(lb[0:64, 0:2, :], lf[0:64, 0:2, :], AF.Copy, [lf], [lb])
                    act(lb[:, 2, :], lf[:, 2, :], AF.Copy, [lf], [lb])

                    gS = em.sb("gS", [64, 4 * W], F32, mes); gSb = em.sb("gSb", [64, 4 * W], BF16, mes)
                    sS = em.sb("sS", [128, 512], F32, mes); sSb = em.sb("sSb", [128, 512], BF16, mes)
                    rH = em.sb("rH", [64, 4 * W], F32, mes); rHb = em.sb("rHb", [64, 4 * W], BF16, mes)
                    if SO:
                        for b_ in (gS, sS, rH):
                            em.op('pool', lambda e, b_=b_: e.memset(b_[:], 0.0), [], [b_])
                        for b_ in (gS, rH):
                            for h in range(4):
                                em.op('pool', lambda e, b_=b_, h=h: e.tensor_copy(out=b_[:, h * W + 64:(h + 1) * W], in_=IDF[0:64, 0:64]),
                                      [cst, b_], [b_])
                        s_tot = em.sb("s_tot", [128, 8], F32, mes)
                        em.op('pool', lambda e: e.memset(s_tot[:], 0.0), [], [s_tot])
                    else:
                        for (b_, src_) in ((gS, gSin), (sS, sSin), (rH, rSin)):
                            em.op('pool', lambda e, b_=b_, src_=src_: e.tensor_copy(out=b_[:], in_=src_[:]), [src_], [b_])
                    for (f_, b_) in ((gS, gSb), (sS, sSb), (rH, rHb)):
                        em.op('pool', lambda e, f_=f_, b_=b_: e.tensor_copy(out=b_[:], in_=f_[:]), [f_], [b_])
                    stio = em.sb("stio", [64, 1024], F32, mes)

                    xs = [em.sb("xs%d" % d_, [128, 1792], F32, mes) for d_ in range(4)]
                    pz = em.sb("pz", [128, 776], F32, mes)
                    pr = em.sb("pr", [128, 1032], F32, mes)
                    prv = em.sb("prv", [128, 1024], F32, mes)
                    if not SO:
                        mix = em.sb("mix", [128, D], F32, mes)
                        mixb = em.sb("mixb", [128, D], BF16, mes)
                        mixT = em.sb("mixT", [128, 8, 128], BF16, mes)

                    cnt = [0]

                    def T(shape_cols, dt=F32, rows=128):
                        cnt[0] += 1
                        return em.sb("t%d_%d" % (l, cnt[0]), [rows, shape_cols], dt, mes)

                    g_sq = T(512); g_ss = T(16); g_qk = T(512); g_qkb = T(512, BF16)
                    g_gt = T(32)
                    g_bd = T(8, F32, 64)
                    g_kb = T(256); g_kbb = T(256, BF16); g_qdb = T(256, BF16); g_kd = [T(256, BF16) for _ in range(2)]
                    g_kbg = T(256, BF16); g_vb = T(256, BF16)
                    g_fT = T(16 * 128, BF16, 64)
                    g_Gs = T(512); g_Dm = T(512); g_DmS = T(512)
                    g_attnT = T(512, BF16); g_LT = T(512, BF16); g_L = T(512, BF16)
                    nA = [T(512, BF16) for _ in range(2)]; nAT = [T(512, BF16) for _ in range(2)]; nP = [T(512, BF16) for _ in range(2)]
                    g_nwT = T(512, BF16, 64)
                    g_vn = T(4 * W, BF16); g_o = T(256)
                    s_t = T(64)
                    s_bd = T(16)
                    s_xdt = T(512, BF16); s_xde = [T(512, BF16) for _ in range(2)]; s_xD = g_Dm
                    s_BCb = T(512, BF16); s_BCT = T(512, BF16)
                    big1 = T(1024); s_Gs = big1; s_E = prv; s_BCm = T(256); s_scT = T(1024, BF16)
                    s_y = g_DmS; s_tmp = g_Gs
                    s_ss = T(8)
                    r_xm = big1; r_lin = T(256, BF16); r_linT = T(256, BF16)
                    r_lw = T(256); r_icl = T(256); r_gate = T(256); r_tmp = T(256); r_tmp2 = T(256)
                    r_kk = T(256); r_k2 = T(256); r_ss = T(16)
                    r_cs = T(256); r_e = T(256)
                    r_at = T(256, BF16); r_rt = T(256, BF16); r_bt = T(256, BF16); r_kt = T(256, BF16)
                    r_bh = [T(256, BF16) for _ in range(2)]; r_kh = [T(256, BF16) for _ in range(2)]
                    r_vb = T(256, BF16)
                    r_fT = g_fT
                    r_AB = s_scT; r_AK = T(1024, BF16)
                    r_A0 = g_LT; r_A0T = g_L
                    r_gC = T(8, F32, 64)
                    r_X = T(4 * W, BF16); r_U = T(4 * W, BF16); r_y = T(256)

                    for b_ in (g_vn, r_X, r_U):
                        em.op('pool', lambda e, b_=b_: e.memset(b_[:], 0.0), [], [b_])

                    def neumann(A0, A0T):
                        Pc = nP[0]
                        tt('pool', Pc[:].rearrange("p (h i) -> p h i", h=4), A0[:].rearrange("p (h i) -> p h i", h=4),
                           identb[:].unsqueeze(1).to_broadcast([128, 4, 128]), ALU.add, [A0, identb], [Pc])
                        Ac, ATc = A0, A0T
                        for lev in range(1, 6):
                            An, ATn = nA[lev % 2], nAT[lev % 2]
                            pb1 = P()
                            for h in range(4):
                                hs = slice(h * 128, (h + 1) * 128)
                                mm(pb1[:, hs], Ac[:, hs], ATc[:, hs], [Ac, ATc], [pb1])
                            act(ATn[:], pb1[:], AF.Copy, [pb1], [ATn])
                            if lev < 5:
                                pb2 = P()
                                for h in range(4):
                                    hs = slice(h * 128, (h + 1) * 128)
                                    mm(pb2[:, hs], ATc[:, hs], Ac[:, hs], [Ac, ATc], [pb2])
                                em.op('dve', lambda e, An=An, pb2=pb2: e.tensor_copy(out=An[:], in_=pb2[:]), [pb2], [An])
                            pb3 = P()
                            for h in range(4):
                                hs = slice(h * 128, (h + 1) * 128)
                                mm(pb3[:, hs], ATn[:, hs], Pc[:, hs], [ATn, Pc], [pb3])
                            Pn = nP[lev % 2]
                            tt('dve', Pn[:], pb3[:], Pc[:], ALU.add, [pb3, Pc], [Pn])
                            Pc, Ac, ATc = Pn, An, ATn
                        return Pc

                    def transpose_heads(dst, dst_idx, src, col0, srcbuf):
                        pb = P()
                        for h in range(4):
                            mm(pb[0:64, h * 128:(h + 1) * 128], src[:, col0 + h * 64: col0 + (h + 1) * 64], identb[:],
                               [srcbuf, identb], [pb])
                        return pb

                    for t in (range(PT) if SO else range(NT)):
                        is_s = t >= PT
                        for c in range(2):
                            if t < PT and c == 1:
                                continue
                            r0 = seq_rows(t, c)
                            nr = 128 if t < PT else 64
                            rs = slice(c * 64, c * 64 + nr)
                            for d_ in range(4):
                                LD(xs[d_][rs, 0:768], projs[r0 - d_:r0 - d_ + nr, 0:768], [projs], [xs[d_]])
                                LD(xs[d_][rs, 768:1792], projs[r0 - d_:r0 - d_ + nr, SX:SX + 1024], [projs], [xs[d_]])
                            LD(pz[rs, :], projs[r0:r0 + nr, 768:1544], [projs], [pz])
                            LD(pr[rs, :], projs[r0:r0 + nr, 2568:3600], [projs], [pr])

                        for d_ in range(4):
                            tt('pool', xs[d_][:], xs[d_][:], cwt[:, 3 - d_, :], ALU.mult, [xs[d_], cwt], [xs[d_]])
                        tt('dve', xs[0][:], xs[0][:], xs[1][:], ALU.add, [xs[0], xs[1]], [xs[0]])
                        tt('dve', xs[2][:], xs[2][:], xs[3][:], ALU.add, [xs[2], xs[3]], [xs[2]])
                        tt('dve', xs[0][:], xs[0][:], xs[2][:], ALU.add, [xs[0], xs[2]], [xs[0]])
                        tt('dve', xs[0][:], xs[0][:], cbt[:], ALU.add, [xs[0], cbt], [xs[0]])
                        sigmoid(xs[1][:], xs[0][:], [xs[0]], [xs[1]])
                        tt('pool', xs[0][:], xs[0][:], xs[1][:], ALU.mult, [xs[0], xs[1]], [xs[0]])
                        cv = xs[0]

                        act(g_sq[:], cv[:, 0:512], AF.Square, [cv], [g_sq])
                        red(g_ss[:, 0:8], g_sq[:].rearrange("p (h d) -> p h d", h=8), [g_sq], [g_ss])
                        rsqrt(g_ss[:, 8:16], g_ss[:, 0:8], [g_ss], [g_ss], epsb[:, 0:1])
                        tsc('dve', g_ss[:, 8:12], g_ss[:, 8:12], 0.125, ALU.mult, [g_ss], [g_ss])
                        tt('dve', g_qk[:].rearrange("p (h d) -> p h d", h=8), cv[:, 0:512].rearrange("p (h d) -> p h d", h=8),
                           g_ss[:, 8:16].unsqueeze(2).to_broadcast([128, 8, 64]), ALU.mult, [cv, g_ss], [g_qk])
                        act(g_qkb[:], g_qk[:], AF.Copy, [g_qk], [g_qkb])
                        tt('dve', g_gt[:, 24:28], pz[:, GA - 768:GA - 768 + 4], GDTB, ALU.add, [pz, sm], [g_gt])
                        act(g_gt[:, 24:28], g_gt[:, 24:28], AF.Exp, [g_gt], [g_gt])
                        act(g_gt[:, 24:28], g_gt[:, 24:28], AF.Ln, [g_gt], [g_gt], bias=1.0)
                        stt('dve', g_gt[:, 0:4], g_gt[:, 24:28], -1.0, GAe, ALU.mult, ALU.mult, [g_gt, sm], [g_gt])
                        sigmoid(g_gt[:, 4:8], pz[:, GB_ - 768:GB_ - 768 + 4], [pz], [g_gt])
                        pb = P()
                        mm(pb[:, 0:4], TRI, g_gt[:, 0:4], [cst, g_gt], [pb])
                        mm(pb[:, 4:8], BLK, g_gt[:, 0:4], [cst, g_gt], [pb])
                        for c in range(2):
                            mm(pb[0:64, 8 + 4 * c:12 + 4 * c], ROWC(c)[:, 0:64], g_gt[:, 0:4], [cst, g_gt], [pb])
                        act(g_gt[:, 8:12], pb[:, 0:4], AF.Copy, [pb], [g_gt])
                        act(g_gt[:, 12:16], pb[:, 0:4], AF.Exp, [pb], [g_gt])
                        tt('dve', g_gt[:, 24:28], pb[:, 4:8], g_gt[:, 8:12], ALU.subtract, [pb, g_gt], [g_gt])
                        act(g_gt[:, 24:28], g_gt[:, 24:28], AF.Exp, [g_gt], [g_gt])
                        for c in range(2):
                            tsc('dve', g_gt[:, 16 + 4 * c:20 + 4 * c], g_gt[:, 24:28], CHSEL[:, c:c + 1], ALU.mult, [g_gt, cst], [g_gt])
                        act(g_bd[:], pb[0:64, 8:16], AF.Exp, [pb], [g_bd])
                        def hb(ap4):
                            return ap4.unsqueeze(2).to_broadcast([128, 4, 64])
                        def h3(ap):
                            return ap.rearrange("p (h d) -> p h d", h=4)
                        tt('dve', h3(g_kb[:]), h3(g_qk[:, 256:512]), hb(g_gt[:, 4:8]), ALU.mult, [g_qk, g_gt], [g_kb])
                        act(g_kbb[:], g_kb[:], AF.Copy, [g_kb], [g_kbb])
                        tt('pool', h3(g_qdb[:]), h3(g_qk[:, 0:256]), hb(g_gt[:, 12:16]), ALU.mult, [g_qk, g_gt], [g_qdb])
                        for c in range(2):
                            tt('dve', h3(g_kd[c][:]), h3(g_qk[:, 256:512]), hb(g_gt[:, 16 + 4 * c:20 + 4 * c]), ALU.mult,
                               [g_qk, g_gt], [g_kd[c]])
                        tt('pool', h3(g_kbg[:]), h3(g_kb[:]), hb(g_gt[:, 12:16]), ALU.mult, [g_kb, g_gt], [g_kbg])
                        tt('pool', h3(g_vb[:]), h3(cv[:, 512:768]), hb(g_gt[:, 4:8]), ALU.mult, [cv, g_gt], [g_vb])
                        fT = g_fT[:].rearrange("p (w i) -> p w i", w=16)
                        for w_, (src, col0) in enumerate([(g_qkb, 256), (g_qkb, 0), (g_kbb, 0), (g_qdb, 0)]):
                            pbt = transpose_heads(None, None, src, col0, src)
                            act(g_fT[:, w_ * 512:(w_ + 1) * 512], pbt[0:64, :], AF.Copy, [pbt], [g_fT])
                        kT = lambda h: fT[:, 0 + h, :]
                        qT = lambda h: fT[:, 4 + h, :]
                        kbT = lambda h: fT[:, 8 + h, :]
                        qdT = lambda h: fT[:, 12 + h, :]
                        tt('pool', g_Gs[:].rearrange("p (h j) -> p h j", h=4), SUPS.unsqueeze(1).to_broadcast([128, 4, 128]),
                           g_gt[:, 0:4].unsqueeze(2).to_broadcast([128, 4, 128]), ALU.mult, [cst, g_gt], [g_Gs])
                        pbD = P()
                        for h in range(4):
                            mm(pbD[:, h * 128:(h + 1) * 128], g_Gs[:, h * 128:(h + 1) * 128], TLE, [g_Gs, cst], [pbD])
                        act(g_Dm[:], pbD[:], AF.Exp, [pbD], [g_Dm])
                        tt('pool', g_DmS[:].rearrange("p (h i) -> p h i", h=4), g_Dm[:].rearrange("p (h i) -> p h i", h=4),
                           MSTR.unsqueeze(1).to_broadcast([128, 4, 128]), ALU.mult, [g_Dm, cst], [g_DmS])
                        tt('pool', g_Dm[:].rearrange("p (h i) -> p h i", h=4), g_Dm[:].rearrange("p (h i) -> p h i", h=4),
                           TRI.unsqueeze(1).to_broadcast([128, 4, 128]), ALU.mult, [g_Dm, cst], [g_Dm])
                        pbQ = P(); pbK = P()
                        for h in range(4):
                            hs = slice(h * 128, (h + 1) * 128)
                            mm(pbQ[:, hs], kT(h), qT(h), [g_fT], [pbQ])
                            mm(pbK[:, hs], kT(h), kbT(h), [g_fT], [pbK])
                        tt('dve', g_attnT[:], pbQ[:], g_Dm[:], ALU.mult, [pbQ, g_Dm], [g_attnT])
                        stt('dve', g_LT[:], pbK[:], -1.0, g_DmS[:], ALU.mult, ALU.mult, [pbK, g_DmS], [g_LT])
                        pbT = P()
                        for h in range(4):
                            hs = slice(h * 128, (h + 1) * 128)
                            mm(pbT[:, hs], g_LT[:, hs], identb[:], [g_LT, identb], [pbT])
                        act(g_L[:], pbT[:], AF.Copy, [pbT], [g_L])
                        TTg = neumann(g_LT, g_L)
                        pbW = P()
                        for h in range(4):
                            mm(pbW[0:64, h * 128:(h + 1) * 128], g_kbg[:, h * 64:(h + 1) * 64], TTg[:, h * 128:(h + 1) * 128],
                               [g_kbg, TTg], [pbW])
                        act(g_nwT[:], pbW[0:64, :], AF.Copy, [pbW], [g_nwT], scale=-1.0)
                        for c in range(2):
                            cs_ = slice(c * 64, (c + 1) * 64)
                            if is_s:
                                s_id = 2 * (t - PT) + c
                                LD(gS[:].rearrange("k (h v) -> k h v", h=4), st_gdn[l, s_id].rearrange("h k v -> k h v"), [st_gdn], [gS])
                                act(gSb[:], gS[:], AF.Copy, [gS], [gSb])
                            pv = P()
                            for h in range(4):
                                vs = slice(h * 64, (h + 1) * 64)
                                ws = slice(h * W, (h + 1) * W)
                                rs_ = slice(h * W, h * W + 64)
                                mm(pv[:, ws], g_nwT[:, h * 128:(h + 1) * 128], gSb[:, ws], [g_nwT, gSb], [pv], start=True, stop=False)
                                mm(pv[:, rs_], TTg[:, h * 128:(h + 1) * 128], g_vb[:, vs], [TTg, g_vb], [pv], start=False, stop=True)
                            act(g_vn[cs_, :], pv[cs_, 0:4 * W], AF.Copy, [pv], [g_vn])
                            pS = P()
                            if not SO:
                                po = P()
                            for h in range(4):
                                vs = slice(h * 64, (h + 1) * 64)
                                ws = slice(h * W, (h + 1) * W)
                                if not SO:
                                    mm(po[:, vs], qdT(h), gSb[:, vs], [g_fT, gSb], [po], start=True, stop=False)
                                    mm(po[:, vs], g_attnT[:, h * 128:(h + 1) * 128], g_vn[:, vs], [g_attnT, g_vn], [po], start=False, stop=True)
                                mm(pS[0:64, ws], g_kd[c][:, vs], g_vn[:, ws], [g_kd[c], g_vn], [pS])
                            if not SO:
                                act(g_o[cs_, :], po[cs_, 0:256], AF.Copy, [po], [g_o])
                            tt('dve', gS[:].rearrange("k (h v) -> k h v", h=4), gS[:].rearrange("k (h v) -> k h v", h=4),
                               g_bd[:, 4 * c:4 * c + 4].unsqueeze(2).to_broadcast([64, 4, W]), ALU.mult, [gS, g_bd], [gS])
                            tt('dve', gS[:], gS[:], pS[0:64, 0:4 * W], ALU.add, [gS, pS], [gS])
                            act(gSb[:], gS[:], AF.Copy, [gS], [gSb])
                            if (not SO) and (is_s or (t == PT - 1 and c == 1)):
                                dst = o_sgdn[l, 2 * (t - PT) + c] if is_s else o_pgdn[l]
                                dbuf = o_sgdn if is_s else o_pgdn
                                ST(dst.rearrange("h k v -> k h v"), gS[:].rearrange("k (h v) -> k h v", h=4), [gS], [dbuf])
                        if not SO:
                            act(g_sq[:, 0:256], g_o[:], AF.Square, [g_o], [g_sq])
                            red(g_ss[:, 0:4], g_sq[:, 0:256].rearrange("p (h d) -> p h d", h=4), [g_sq], [g_ss])
                            rsqrt(g_ss[:, 4:8], g_ss[:, 0:4], [g_ss], [g_ss], epsb[:, 0:1], scale=1.0 / 64)
                            tt('dve', h3(g_o[:]), h3(g_o[:]), hb(g_ss[:, 4:8]), ALU.mult, [g_o, g_ss], [g_o])
                            tt('pool', g_o[:], g_o[:], gnw[:], ALU.mult, [g_o, gnw], [g_o])
                            sigmoid(g_sq[:, 256:512], pz[:, 0:256], [pz], [g_sq])
                            tt('pool', g_o[:], g_o[:], pz[:, 0:256], ALU.mult, [g_o, pz], [g_o])
                            tt('dve', mix[:, 0:256], g_o[:], g_sq[:, 256:512], ALU.mult, [g_o, g_sq], [mix])

                        XO, BO, CO = 768, 1280, 1536
                        tt('dve', s_t[:, 48:56], pr[:, 0:8], SDTB, ALU.add, [pr, sm], [s_t])
                        act(s_t[:, 48:56], s_t[:, 48:56], AF.Exp, [s_t], [s_t])
                        act(s_t[:, 0:8], s_t[:, 48:56], AF.Ln, [s_t], [s_t], bias=1.0)
                        stt('dve', s_t[:, 8:16], s_t[:, 0:8], -1.0, SAe, ALU.mult, ALU.mult, [s_t, sm], [s_t])
                        pb = P()
                        mm(pb[:, 0:8], TRI, s_t[:, 8:16], [cst, s_t], [pb])
                        mm(pb[:, 8:16], BLK, s_t[:, 8:16], [cst, s_t], [pb])
                        for c in range(2):
                            mm(pb[:, 16 + 8 * c:24 + 8 * c], ROWC(c), s_t[:, 8:16], [cst, s_t], [pb])
                        act(s_t[:, 16:24], pb[:, 0:8], AF.Copy, [pb], [s_t])
                        act(s_t[:, 24:32], pb[:, 0:8], AF.Exp, [pb], [s_t])
                        tt('dve', s_t[:, 48:56], pb[:, 8:16], s_t[:, 16:24], ALU.subtract, [pb, s_t], [s_t])
                        act(s_t[:, 48:56], s_t[:, 48:56], AF.Exp, [s_t], [s_t])
                        for c in range(2):
                            tsc('dve', s_t[:, 32 + 8 * c:40 + 8 * c], s_t[:, 48:56], CHSEL[:, c:c + 1], ALU.mult, [s_t, cst], [s_t])
                        act(s_bd[:], pb[:, 16:32], AF.Exp, [pb], [s_bd])
                        if SO:
                            tt('dve', s_tot[:], s_tot[:], pb[:, 16:24], ALU.add, [s_tot, pb], [s_tot])
                            tt('dve', s_tot[:], s_tot[:], pb[:, 24:32], ALU.add, [s_tot, pb], [s_tot])
                        def h8(ap):
                            return ap.rearrange("p (h d) -> p h d", h=8)
                        def hb8(ap8):
                            return ap8.unsqueeze(2).to_broadcast([128, 8, 64])
                        tt('dve', h8(s_y[:]), h8(cv[:, XO:XO + 512]), hb8(s_t[:, 0:8]), ALU.mult, [cv, s_t], [s_y])
                        if not SO:
                            act(s_xdt[:], s_y[:], AF.Copy, [s_y], [s_xdt])
                        for c in range(2):
                            tt('pool', h8(s_xde[c][:]), h8(s_y[:]), hb8(s_t[:, 32 + 8 * c:40 + 8 * c]), ALU.mult, [s_y, s_t], [s_xde[c]])
                        if not SO:
                            tt('pool', h8(s_xD[:]), h8(cv[:, XO:XO + 512]), hb8(SDD), ALU.mult, [cv, sm], [s_xD])
                        act(s_BCb[:], cv[:, BO:BO + 512], AF.Copy, [cv], [s_BCb])
                        if not SO:
                            pbt = P()
                            for j in range(4):
                                mm(pbt[:, j * 128:(j + 1) * 128], s_BCb[:, j * 128:(j + 1) * 128], identb[:], [s_BCb, identb], [pbt])
                            act(s_BCT[:], pbt[:], AF.Copy, [pbt], [s_BCT])
                            pbb = P()
                            for g_ in range(2):
                                mm(pbb[:, g_ * 128:(g_ + 1) * 128], s_BCT[:, g_ * 128:(g_ + 1) * 128],
                                   s_BCT[:, (2 + g_) * 128:(3 + g_) * 128], [s_BCT], [pbb])
                            tt('dve', s_BCm[:].rearrange("p (g i) -> p g i", g=2), pbb[:, 0:256].rearrange("p (g i) -> p g i", g=2),
                               TRI.unsqueeze(1).to_broadcast([128, 2, 128]), ALU.mult, [pbb, cst], [s_BCm])
                            tt('pool', s_Gs[:].rearrange("p (h j) -> p h j", h=8), SUPS.unsqueeze(1).to_broadcast([128, 8, 128]),
                               s_t[:, 8:16].unsqueeze(2).to_broadcast([128, 8, 128]), ALU.mult, [cst, s_t], [s_Gs])
                            for half in range(2):
                                pbD = P()
                                for hh in range(4):
                                    h = half * 4 + hh
                                    mm(pbD[:, hh * 128:(hh + 1) * 128], s_Gs[:, h * 128:(h + 1) * 128], TLE, [s_Gs, cst], [pbD])
                                act(s_E[:, half * 512:(half + 1) * 512], pbD[:], AF.Exp, [pbD], [s_E])
                                tt('dve', s_scT[:, half * 512:(half + 1) * 512].rearrange("p (h i) -> p h i", h=4),
                                   s_E[:, half * 512:(half + 1) * 512].rearrange("p (h i) -> p h i", h=4),
                                   s_BCm[:, half * 128:(half + 1) * 128].unsqueeze(1).to_broadcast([128, 4, 128]), ALU.mult,
                                   [s_E, s_BCm], [s_scT])
                            pyd = P()
                            for h in range(8):
                                mm(pyd[:, h * 64:(h + 1) * 64], s_scT[:, h * 128:(h + 1) * 128], s_xdt[:, h * 64:(h + 1) * 64],
                                   [s_scT, s_xdt], [pyd])
                            tt('dve', s_y[:], pyd[:], s_xD[:], ALU.add, [pyd, s_xD], [s_y])
                        for c in range(2):
                            cs_ = slice(c * 64, (c + 1) * 64)
                            if is_s:
                                s_id = 2 * (t - PT) + c
                                LD(stio[:, :].rearrange("p (h n) -> p h n", h=8), st_ssm[l, s_id].rearrange("h p n -> p h n"), [st_ssm], [stio])
                                for half in range(2):
                                    pbs = P()
                                    for hh in range(4):
                                        h = half * 4 + hh
                                        mm(pbs[:, hh * 64:(hh + 1) * 64], stio[:, h * 128:(h + 1) * 128], IDF[0:64, 0:64], [stio, cst], [pbs])
                                    act(sS[:, half * 256:(half + 1) * 256], pbs[:, 0:256], AF.Copy, [pbs], [sS])
                                act(sSb[:], sS[:], AF.Copy, [sS], [sSb])
                            pcs = P()
                            if not SO:
                                pyo = P()
                            for g_ in range(2):
                                gs = slice(g_ * 256, (g_ + 1) * 256)
                                if not SO:
                                    mm(pyo[:, gs], s_BCT[:, (2 + g_) * 128:(3 + g_) * 128], sSb[:, gs], [s_BCT, sSb], [pyo])
                                mm(pcs[:, gs], s_BCb[:, g_ * 128:(g_ + 1) * 128], s_xde[c][:, gs], [s_BCb, s_xde[c]], [pcs])
                            if not SO:
                                tt('dve', h8(s_tmp[cs_, :]), h8(pyo[cs_, :]), s_t[cs_, 24:32].unsqueeze(2).to_broadcast([64, 8, 64]), ALU.mult,
                                   [pyo, s_t], [s_tmp])
                                tt('pool', s_y[cs_, :], s_y[cs_, :], s_tmp[cs_, :], ALU.add, [s_y, s_tmp], [s_y])
                            tt('dve', h8(sS[:]), h8(sS[:]), s_bd[:, 8 * c:8 * c + 8].unsqueeze(2).to_broadcast([128, 8, 64]), ALU.mult,
                               [sS, s_bd], [sS])
                            tt('dve', sS[:], sS[:], pcs[:], ALU.add, [sS, pcs], [sS])
                            act(sSb[:], sS[:], AF.Copy, [sS], [sSb])
                            if not SO:
                                if is_s or (t == PT - 1 and c == 1):
                                    dst = o_sssm[l, 2 * (t - PT) + c] if is_s else o_pssm[l]
                                    dbuf = o_sssm if is_s else o_pssm
                                    for half in range(2):
                                        pbs = P()
                                        for hh in range(4):
                                            h = half * 4 + hh
                                            mm(pbs[0:64, hh * 128:(hh + 1) * 128], sS[:, h * 64:(h + 1) * 64], IDF, [sS, cst], [pbs])
                                        act(stio[:, half * 512:(half + 1) * 512], pbs[0:64, :], AF.Copy, [pbs], [stio])
                                    ST(dst.rearrange("h p n -> p h n"), stio[:, :].rearrange("p (h n) -> p h n", h=8), [stio], [dbuf])
                        if not SO:
                            sigmoid(s_tmp[:], pz[:, SZ - 768:SZ - 768 + 512], [pz], [s_tmp])
                            tt('pool', s_y[:], s_y[:], pz[:, SZ - 768:SZ - 768 + 512], ALU.mult, [s_y, pz], [s_y])
                            tt('dve', s_y[:], s_y[:], s_tmp[:], ALU.mult, [s_y, s_tmp], [s_y])
                            act(s_tmp[:], s_y[:], AF.Square, [s_y], [s_tmp])
                            red(s_ss[:, 0:2], s_tmp[:].rearrange("p (g d) -> p g d", g=2), [s_tmp], [s_ss])
                            rsqrt(s_ss[:, 2:4], s_ss[:, 0:2], [s_ss], [s_ss], epsb[:, 0:1], scale=1.0 / 256)
                            tt('dve', s_y[:].rearrange("p (g d) -> p g d", g=2), s_y[:].rearrange("p (g d) -> p g d", g=2),
                               s_ss[:, 2:4].unsqueeze(2).to_broadcast([128, 2, 256]), ALU.mult, [s_y, s_ss], [s_y])
                            tt('pool', mix[:, 256:768], s_y[:], snw[:], ALU.mult, [s_y, snw], [mix])

                        for c in range(2):
                            if t < PT and c == 1:
                                continue
                            r0 = seq_rows(t, c)
                            nr = 128 if t < PT else 64
                            LD(prv[c * 64:c * 64 + nr, :], projs[r0 - 1:r0 - 1 + nr, RR:RR + 1024], [projs], [prv])
                        pc = pr[:, 8:1032]
                        tt('pool', prv[:], prv[:], pc, ALU.subtract, [prv, pr], [prv])
                        tt('pool', prv[:], prv[:], mut[:], ALU.mult, [prv, mut], [prv])
                        tt('dve', r_xm[:], prv[:], pc, ALU.add, [prv, pr], [r_xm])
                        Rr, Rk, Rv = r_xm[:, 0:256], r_xm[:, 256:512], r_xm[:, 512:768]
                        sigmoid(r_tmp[:, 0:64], r_xm[:, 768:832], [r_xm], [r_tmp], scale=2.0)
                        tsc('dve', r_lin[:, 0:64], r_tmp[:, 0:64], 2.0, ALU.mult, [r_tmp], [r_lin], s2=-1.0, op1=ALU.add)
                        act(r_lin[:, 64:128], r_xm[:, 832:896], AF.Copy, [r_xm], [r_lin])
                        sigmoid(r_tmp[:, 64:192], r_xm[:, 896:1024], [r_xm], [r_tmp])
                        act(r_lin[:, 128:256], r_tmp[:, 64:192], AF.Copy, [r_tmp], [r_lin])
                        pbt = P()
                        mm(pbt[0:64, 0:128], r_lin[:, 0:64], identb[:], [r_lin, identb], [pbt])
                        mm(pbt[0:64, 128:256], r_lin[:, 64:128], identb[:], [r_lin, identb], [pbt])
                        mm(pbt[:, 256:384], r_lin[:, 128:256], identb[:], [r_lin, identb], [pbt])
                        r_linT2 = r_linT
                        act(r_linT[0:64, :], pbt[0:64, 0:256], AF.Copy, [pbt], [r_linT])
                        act(g_sq[:, 0:128], pbt[:, 256:384], AF.Copy, [pbt], [g_sq])
                        act(r_at[:, 0:128], g_sq[:, 0:128], AF.Copy, [g_sq], [r_at])
                        pl = P(); pl2 = P()
                        mm(pl[:, 0:256], r_linT[0:64, 0:128], lb[0:64, 0, :], [r_linT, lb], [pl])
                        mm(pl[:, 256:512], r_linT[0:64, 128:256], lb[0:64, 1, :], [r_linT, lb], [pl])
                        mm(pl2[:, 0:256], r_at[:, 0:128], lb[:, 2, :], [r_at, lb], [pl2])
                        tt('dve', r_tmp[:], pl[:, 0:256], W0, ALU.add, [pl, rv], [r_tmp])
                        sigmoid(r_lw[:], r_tmp[:], [r_tmp], [r_lw])
                        tsc('dve', r_lw[:], r_lw[:], -0.6065306597126334, ALU.mult, [r_lw], [r_lw])
                        tt('dve', r_tmp[:], pl[:, 256:512], A0, ALU.add, [pl, rv], [r_tmp])
                        sigmoid(r_icl[:], r_tmp[:], [r_tmp], [r_icl])
                        act(r_gate[:], pl2[:, 0:256], AF.Copy, [pl2], [r_gate])
                        tt('pool', r_kk[:], Rk, KK_, ALU.mult, [r_xm, rv], [r_kk])
                        act(r_tmp[:], r_kk[:], AF.Square, [r_kk], [r_tmp])
                        red(r_ss[:, 0:4], h3(r_tmp[:]), [r_tmp], [r_ss])
                        rsqrt(r_ss[:, 4:8], r_ss[:, 0:4], [r_ss], [r_ss], epsb[:, 0:1])
                        tt('dve', h3(r_kk[:]), h3(r_kk[:]), hb(r_ss[:, 4:8]), ALU.mult, [r_kk, r_ss], [r_kk])
                        stt('dve', r_tmp[:], r_icl[:], -1.0, KA_, ALU.add, ALU.mult, [r_icl, rv], [r_tmp])
                        stt('dve', r_k2[:], r_tmp[:], 1.0, Rk, ALU.add, ALU.mult, [r_tmp, r_xm], [r_k2])
                        pcs = P()
                        mm(pcs[:, 0:256], TRI, r_lw[:], [cst, r_lw], [pcs])
                        mm(pcs[:, 256:512], BLK, r_lw[:], [cst, r_lw], [pcs])
                        pg = P()
                        for h in range(4):
                            mm(pg[0:64, 2 * h:2 * h + 2], r_lw[:, h * 64:(h + 1) * 64], CHSEL, [r_lw, cst], [pg])
                        act(r_gC[:], pg[0:64, 0:8], AF.Exp, [pg], [r_gC])
                        act(r_cs[:], pcs[:, 0:256], AF.Copy, [pcs], [r_cs])
                        act(r_e[:], r_cs[:], AF.Exp, [r_cs], [r_e])
                        tt('dve', r_rt[:], Rr, r_e[:], ALU.mult, [r_xm, r_e], [r_rt])
                        tt('dve', r_tmp[:], r_cs[:], r_lw[:], ALU.subtract, [r_cs, r_lw], [r_tmp])
                        act(r_e[:], r_tmp[:], AF.Exp, [r_tmp], [r_e])
                        stt('dve', r_at[:], r_kk[:], -1.0, r_e[:], ALU.mult, ALU.mult, [r_kk, r_e], [r_at])
                        tt('pool', r_tmp2[:], r_kk[:], r_icl[:], ALU.mult, [r_kk, r_icl], [r_tmp2])
                        act(r_e[:], r_cs[:], AF.Exp, [r_cs], [r_e], scale=-1.0)
                        tt('dve', r_bt[:], r_tmp2[:], r_e[:], ALU.mult, [r_tmp2, r_e], [r_bt])
                        tt('pool', r_kt[:], r_k2[:], r_e[:], ALU.mult, [r_k2, r_e], [r_kt])
                        tt('dve', r_tmp[:], pcs[:, 256:512], r_cs[:], ALU.subtract, [pcs, r_cs], [r_tmp])
                        act(r_e[:], r_tmp[:], AF.Exp, [r_tmp], [r_e])
                        for c in range(2):
                            stt('dve', r_bh[c][:], r_tmp2[:], CHSEL[:, c:c + 1], r_e[:], ALU.mult, ALU.mult, [r_tmp2, cst, r_e], [r_bh[c]])
                            stt('dve', r_kh[c][:], r_k2[:], CHSEL[:, c:c + 1], r_e[:], ALU.mult, ALU.mult, [r_k2, cst, r_e], [r_kh[c]])
                        act(r_vb[:], Rv, AF.Copy, [r_xm], [r_vb])
                        fTr = r_fT[:].rearrange("p (h w i) -> p h w i", h=4, w=4)
                        for w_, src in enumerate([r_at, r_rt, r_bt, r_kt]):
                            pbt = transpose_heads(None, None, src, 0, src)
                            act(fTr[:, :, w_, :], pbt[0:64, :].rearrange("p (h i) -> p h i", h=4), AF.Copy, [pbt], [r_fT])
                        for half in range(2):
                            pab = P(); pak = P()
                            for hh in range(2):
                                h = half * 2 + hh
                                ar = fTr[:, h, 0:2, :]
                                mm(pab[:, hh * 256:(hh + 1) * 256], fTr[:, h, 2, :], ar, [r_fT], [pab])
                                mm(pak[:, hh * 256:(hh + 1) * 256], fTr[:, h, 3, :], ar, [r_fT], [pak])
                            for (pb_, dst_) in ((pab, r_AB), (pak, r_AK)):
                                dv = dst_[:, half * 512:(half + 1) * 512].rearrange("p (h w i) -> p h w i", h=2, w=2)
                                sv = pb_[:].rearrange("p (h w i) -> p h w i", h=2, w=2)
                                tt('dve', dv[:, :, 0, :], sv[:, :, 0, :], MSTR.unsqueeze(1).to_broadcast([128, 2, 128]), ALU.mult, [pb_, cst], [dst_])
                                tt('dve', dv[:, :, 1, :], sv[:, :, 1, :], TRI.unsqueeze(1).to_broadcast([128, 2, 128]), ALU.mult, [pb_, cst], [dst_])
                        ABv = r_AB[:].rearrange("p (h w i) -> p h w i", h=4, w=2)
                        AKv = r_AK[:].rearrange("p (h w i) -> p h w i", h=4, w=2)
                        act(r_A0[:].rearrange("p (h i) -> p h i", h=4), ABv[:, :, 0, :], AF.Copy, [r_AB], [r_A0])
                        pbT = P()
                        for h in range(4):
                            hs = slice(h * 128, (h + 1) * 128)
                            mm(pbT[:, hs], r_A0[:, hs], identb[:], [r_A0, identb], [pbT])
                        act(r_A0T[:], pbT[:], AF.Copy, [pbT], [r_A0T])
                        TTr = neumann(r_A0, r_A0T)
                        for c in range(2):
                            cs_ = slice(c * 64, (c + 1) * 64)
                            if is_s:
                                s_id = 2 * (t - PT) + c
                                LD(stio[:, 0:256].rearrange("p (h k) -> p h k", h=4), st_rwkv[l, s_id].rearrange("h v k -> v h k"), [st_rwkv], [stio])
                                pbs = P()
                                for h in range(4):
                                    mm(pbs[0:64, h * 64:(h + 1) * 64], stio[:, h * 64:(h + 1) * 64], IDF[0:64, 0:64], [stio, cst], [pbs])
                                act(rH[:], pbs[0:64, 0:256], AF.Copy, [pbs], [rH])
                                act(rHb[:], rH[:], AF.Copy, [rH], [rHb])
                            px = P()
                            for h in range(4):
                                vs = slice(h * 64, (h + 1) * 64)
                                ws = slice(h * W, (h + 1) * W)
                                rs_ = slice(h * W, h * W + 64)
                                mm(px[:, ws], fTr[:, h, 0, :], rHb[:, ws], [r_fT, rHb], [px], start=True, stop=False)
                                mm(px[:, rs_], AKv[:, h, 0, :], r_vb[:, vs], [r_AK, r_vb], [px], start=False, stop=True)
                            act(r_X[cs_, :], px[cs_, 0:4 * W], AF.Copy, [px], [r_X])
                            pu = P()
                            for h in range(4):
                                ws = slice(h * W, (h + 1) * W)
                                mm(pu[:, ws], TTr[:, h * 128:(h + 1) * 128], r_X[:, ws], [TTr, r_X], [pu])
                            act(r_U[cs_, :], pu[cs_, 0:4 * W], AF.Copy, [pu], [r_U])
                            ph = P()
                            if not SO:
                                py = P()
                            for h in range(4):
                                vs = slice(h * 64, (h + 1) * 64)
                                ws = slice(h * W, (h + 1) * W)
                                rs_ = slice(h * W, h * W + 64)
                                if not SO:
                                    mm(py[:, vs], fTr[:, h, 1, :], rHb[:, vs], [r_fT, rHb], [py], start=True, stop=False)
                                    mm(py[:, vs], ABv[:, h, 1, :], r_U[:, vs], [r_AB, r_U], [py], start=False, stop=False)
                                    mm(py[:, vs], AKv[:, h, 1, :], r_vb[:, vs], [r_AK, r_vb], [py], start=False, stop=True)
                                mm(ph[0:64, ws], r_bh[c][:, vs], r_U[:, ws], [r_bh[c], r_U], [ph], start=True, stop=False)
                                mm(ph[0:64, rs_], r_kh[c][:, vs], r_vb[:, vs], [r_kh[c], r_vb], [ph], start=False, stop=True)
                            if not SO:
                                act(r_y[cs_, :], py[cs_, 0:256], AF.Copy, [py], [r_y])
                            gCv = r_gC[:].rearrange("k (h c) -> k h c", h=4)
                            tt('dve', rH[:].rearrange("k (h v) -> k h v", h=4), rH[:].rearrange("k (h v) -> k h v", h=4),
                               gCv[:, :, c:c + 1].to_broadcast([64, 4, W]), ALU.mult, [rH, r_gC], [rH])
                            tt('dve', rH[:], rH[:], ph[0:64, 0:4 * W], ALU.add, [rH, ph], [rH])
                            act(rHb[:], rH[:], AF.Copy, [rH], [rHb])
                            if not SO:
                                if is_s or (t == PT - 1 and c == 1):
                                    dst = o_srwkv[l, 2 * (t - PT) + c] if is_s else o_prwkv[l]
                                    dbuf = o_srwkv if is_s else o_prwkv
                                    pbs = P()
                                    for h in range(4):
                                        mm(pbs[0:64, h * 64:(h + 1) * 64], rH[:, h * 64:(h + 1) * 64], IDF[0:64, 0:64], [rH, cst], [pbs])
                                    act(stio[:, 512:768], pbs[0:64, 0:256], AF.Copy, [pbs], [stio])
                                    ST(dst.rearrange("h v k -> v h k"), stio[:, 512:768].rearrange("p (h k) -> p h k", h=4), [stio], [dbuf])
                        if not SO:
                            red(r_ss[:, 0:4], h3(r_y[:]), [r_y], [r_ss])
                            tsc('dve', r_ss[:, 0:4], r_ss[:, 0:4], -1.0 / 64, ALU.mult, [r_ss], [r_ss])
                            tt('dve', h3(r_y[:]), h3(r_y[:]), hb(r_ss[:, 0:4]), ALU.add, [r_y, r_ss], [r_y])
                            act(r_tmp[:], r_y[:], AF.Square, [r_y], [r_tmp])
                            red(r_ss[:, 4:8], h3(r_tmp[:]), [r_tmp], [r_ss])
                            rsqrt(r_ss[:, 8:12], r_ss[:, 4:8], [r_ss], [r_ss], epsb[:, 1:2], scale=1.0 / 64)
                            tt('dve', h3(r_y[:]), h3(r_y[:]), hb(r_ss[:, 8:12]), ALU.mult, [r_y, r_ss], [r_y])
                            tt('pool', r_y[:], r_y[:], LNW, ALU.mult, [r_y, rv], [r_y])
                            tt('pool', r_y[:], r_y[:], LNB, ALU.add, [r_y, rv], [r_y])
                            tt('pool', r_tmp[:], Rr, r_k2[:], ALU.mult, [r_xm, r_k2], [r_tmp])
                            tt('pool', r_tmp[:], r_tmp[:], RK_, ALU.mult, [r_tmp, rv], [r_tmp])
                            red(r_ss[:, 12:16], h3(r_tmp[:]), [r_tmp], [r_ss])
                            tt('dve', h3(r_tmp[:]), h3(Rv), hb(r_ss[:, 12:16]), ALU.mult, [r_xm, r_ss], [r_tmp])
                            tt('dve', r_y[:], r_y[:], r_tmp[:], ALU.add, [r_y, r_tmp], [r_y])
                            tt('dve', mix[:, 768:1024], r_y[:], r_gate[:], ALU.mult, [r_y, r_gate], [mix])

                            act(mixb[:], mix[:], AF.Copy, [mix], [mixb])
                            for half in range(2):
                                pb = P()
                                for kk in range(4):
                                    kc = half * 4 + kk
                                    mm(pb[:, kk * 128:(kk + 1) * 128], mixb[:, kc * 128:(kc + 1) * 128], identb[:], [mixb, identb], [pb])
                                act(mixT[:, half * 4:(half + 1) * 4, :], pb[:].rearrange("p (k i) -> p k i", k=4), AF.Copy, [pb], [mixT])
                            ST(mixTs[t], mixT[:], [mixT], [mixTs])
                    if SO:
                        ST(agin[:, 0:512], sS[:], [sS], [agin])
                        ST(agin[:, 512:520], s_tot[:], [s_tot], [agin])
                        for (st_, c0_, j_) in ((gS, 520, 0), (rH, 1032, 1)):
                            ST(agin[0:64, c0_:c0_ + 256].rearrange("k (h v) -> k h v", h=4),
                               st_[:].rearrange("k (h w) -> k h w", h=4)[:, :, 0:64], [st_], [agin])
                            pbm = P()
                            for h in range(4):
                                mm(pbm[0:64, h * 64:(h + 1) * 64], st_[:, h * W + 64:(h + 1) * W], IDF[0:64, 0:64], [st_, cst], [pbm])
                            act(stio[:, j_ * 256:(j_ + 1) * 256], pbm[0:64, 0:256], AF.Copy, [pbm], [stio])
                            ST(agin[0:64, c0_ + 256:c0_ + 512], stio[:, j_ * 256:(j_ + 1) * 256], [stio], [agin])
                        if os.environ.get('NO_CC'):
                            for r_ in range(8):
                                ST(agout[r_ * 128:(r_ + 1) * 128, :], agin[:], [agin], [agout])
                        else:
                            em.coll(ccs_l[l], [agin], [agout],
                                    lambda e: e.collective_compute("AllGather", ALU.bypass, replica_groups=[list(range(8))],
                                                                   ins=[agin.t.ap().opt()], outs=[agout.t.ap().opt()]))
                    em.end_phase(em.scope_bufs.get(id(mes), []))

            def combine():
                with ExitStack() as ces:
                    agall = em.sb("agall", [128, 8, AGW], F32, ces)
                    for r in range(8):
                        LD(agall[:, r, :], agout[r * 128:(r + 1) * 128, :], [agout], [agall])
                    oh = em.sb("oh", [128, 8], F32, ces)
                    LD(oh[:], oh_me[:], [oh_me], [oh])
                    dtot = em.sb("dtot", [128, 8, 8], F32, ces)
                    act(dtot[:], agall[:, :, 512:520], AF.Exp, [agall], [dtot])
                    gcur = em.sb("gcur", [64, 256], F32, ces); rcur = em.sb("rcur", [64, 256], F32, ces)
                    scur = em.sb("scur", [128, 512], F32, ces)
                    for b_ in (gcur, rcur, scur, gSin, rSin, sSin):
                        em.op('pool', lambda e, b_=b_: e.memset(b_[:], 0.0), [], [b_])
                    for r in range(8):
                        stt('dve', gSin[:], gcur[:], oh[0:64, r:r + 1], gSin[:], ALU.mult, ALU.add, [gcur, oh, gSin], [gSin])
                        stt('dve', rSin[:], rcur[:], oh[0:64, r:r + 1], rSin[:], ALU.mult, ALU.add, [rcur, oh, rSin], [rSin])
                        stt('dve', sSin[:], scur[:], oh[:, r:r + 1], sSin[:], ALU.mult, ALU.add, [scur, oh, sSin], [sSin])
                        if r == 7:
                            break
                        pbg = P()
                        for h in range(4):
                            hs_ = slice(h * 64, (h + 1) * 64)
                            mm(pbg[0:64, h * 64:(h + 1) * 64], agall[0:64, r, 776 + h * 64:776 + (h + 1) * 64], gcur[:, hs_],
                               [agall, gcur], [pbg])
                            mm(pbg[0:64, 256 + h * 64:256 + (h + 1) * 64], agall[0:64, r, 1288 + h * 64:1288 + (h + 1) * 64], rcur[:, hs_],
                               [agall, rcur], [pbg])
                        tt('dve', gcur[:], pbg[0:64, 0:256], agall[0:64, r, 520:776], ALU.add, [pbg, agall], [gcur])
                        tt('dve', rcur[:], pbg[0:64, 256:512], agall[0:64, r, 1032:1288], ALU.add, [pbg, agall], [rcur])
                        tt('dve', scur[:].rearrange("p (h d) -> p h d", h=8), scur[:].rearrange("p (h d) -> p h d", h=8),
                           dtot[:, r, :].unsqueeze(2).to_broadcast([128, 8, 64]), ALU.mult, [scur, dtot], [scur])
                        tt('dve', scur[:], scur[:], agall[:, r, 0:512], ALU.add, [scur, agall], [scur])
                    em.end_phase(em.scope_bufs.get(id(ces), []))

            if shard:
                phase_M("state")
                combine()
            elif l == 0:
                for b_ in (gSin, rSin, sSin):
                    em.op('pool', lambda e, b_=b_: e.memset(b_[:], 0.0), [], [b_])
            phase_M("full")

            with ExitStack() as bes:
                woutr = em.sb("woutr", [128, 8, D], BF16, bes)
                wgr = em.sb("wgr", [128, 8, FH], BF16, bes)
                wur = em.sb("wur", [128, 8, FH], BF16, bes)
                wdr = em.sb("wdr", [128, NJ, D], BF16, bes)
                for kk in range(8):
                    LD(woutr[:, kk, :], wout_b[l, kk * 128:(kk + 1) * 128, :], [wout_b], [woutr])
                    LD(wgr[:, kk, :], wg_b[l, kk * 128:(kk + 1) * 128, :], [wg_b], [wgr])
                    LD(wur[:, kk, :], wu_b[l, kk * 128:(kk + 1) * 128, :], [wu_b], [wur])
                for j in range(NJ):
                    LD(wdr[:, j, :], wd_b[l, j * 128:(j + 1) * 128, :], [wd_b], [wdr])
                n2w = em.sb("n2w", [128, 8], F32, bes)
                LD(n2w[:], norm2_w[l], [norm2_w], [n2w])
                fnw = em.sb("fnw", [128, D], F32, bes)
                LD(fnw[:], final_norm_w[0:1, :].partition_broadcast(128), [], [fnw])
                xt = [em.sb("bxt%d" % i, [128, D], F32, bes) for i in range(2)]
                mT = [em.sb("bmT%d" % i, [128, 8, 128], BF16, bes) for i in range(2)]
                junk = em.sb("bjunk", [128, D], F32, bes)
                ssq = em.sb("bssq", [128, 4], F32, bes)
                hb2 = em.sb("bhb", [128, D], BF16, bes)
                h2T = em.sb("bh2T", [128, 8, 128], BF16, bes)
                sg = em.sb("bsg", [128, 512], F32, bes)
                ffT = em.sb("bffT", [128, NJ, 128], BF16, bes)
                for t in range(NT):
                    q = t % 2
                    LD(xt[q][:], xsrc[t * 128:(t + 1) * 128, :], [xsrc], [xt[q]])
                    LD(mT[q][:], mixTs[t], [mixTs], [mT[q]])
                    for cb_ in range(2):
                        pb = P()
                        for kc in range(8):
                            mm(pb[:], mT[q][:, kc, :], woutr[:, kc, cb_ * 512:(cb_ + 1) * 512], [mT[q], woutr], [pb],
                               start=(kc == 0), stop=(kc == 7))
                        tt('dve', xt[q][:, cb_ * 512:(cb_ + 1) * 512], xt[q][:, cb_ * 512:(cb_ + 1) * 512], pb[:], ALU.add,
                           [xt[q], pb], [xt[q]])
                    act(junk[:], xt[q][:], AF.Square, [xt[q]], [junk, ssq], accum=ssq[:, 0:1])
                    rsqrt(ssq[:, 1:2], ssq[:, 0:1], [ssq], [ssq], epsb[:, 0:1], scale=1.0 / D)
                    act(hb2[:], xt[q][:], AF.Copy, [xt[q], ssq], [hb2], scale=ssq[:, 1:2])
                    for half in range(2):
                        pb = P()
                        for kk in range(4):
                            kc = half * 4 + kk
                            mm(pb[:, kk * 128:(kk + 1) * 128], hb2[:, kc * 128:(kc + 1) * 128], identb[:], [hb2, identb], [pb])
                        for kk in range(4):
                            kc = half * 4 + kk
                            tsc('dve', h2T[:, kc, :], pb[:, kk * 128:(kk + 1) * 128], n2w[:, kc:kc + 1], ALU.mult, [pb, n2w], [h2T])
                    for j0 in range(0, NJ, 4):
                        nj = min(4, NJ - j0)
                        pg_ = P(); pu_ = P()
                        for jj in range(nj):
                            j = j0 + jj
                            for kc in range(8):
                                mm(pg_[:, jj * 128:(jj + 1) * 128], wgr[:, kc, j * 128:(j + 1) * 128], h2T[:, kc, :], [wgr, h2T], [pg_],
                                   start=(kc == 0), stop=(kc == 7))
                            for kc in range(8):
                                mm(pu_[:, jj * 128:(jj + 1) * 128], wur[:, kc, j * 128:(j + 1) * 128], h2T[:, kc, :], [wur, h2T], [pu_],
                                   start=(kc == 0), stop=(kc == 7))
                        w_ = nj * 128
                        sigmoid(sg[:, 0:w_], pg_[:, 0:w_], [pg_], [sg])
                        tt('dve', sg[:, 0:w_], sg[:, 0:w_], pg_[:, 0:w_], ALU.mult, [sg, pg_], [sg])
                        tt('dve', ffT[:, j0:j0 + nj, :], sg[:, 0:w_].rearrange("p (j i) -> p j i", j=nj),
                           pu_[:, 0:w_].rearrange("p (j i) -> p j i", j=nj), ALU.mult, [sg, pu_], [ffT])
                    for cb_ in range(2):
                        pb = P()
                        for j in range(NJ):
                            mm(pb[:], ffT[:, j, :], wdr[:, j, cb_ * 512:(cb_ + 1) * 512], [ffT, wdr], [pb],
                               start=(j == 0), stop=(j == NJ - 1))
                        tt('dve', xt[q][:, cb_ * 512:(cb_ + 1) * 512], xt[q][:, cb_ * 512:(cb_ + 1) * 512], pb[:], ALU.add,
                           [xt[q], pb], [xt[q]])
                    if l == 0:
                        ST(x1s[t * 128:(t + 1) * 128, :], xt[q][:], [xt[q]], [x1s])
                        if t == PT - 1 and shard:
                            ST(hin[:], xt[q][125:128, :], [xt[q]], [hin])
                    else:
                        act(junk[:], xt[q][:], AF.Square, [xt[q]], [junk, ssq], accum=ssq[:, 2:3])
                        rsqrt(ssq[:, 3:4], ssq[:, 2:3], [ssq], [ssq], epsb[:, 0:1], scale=1.0 / D)
                        stt('dve', xt[q][:], xt[q][:], ssq[:, 3:4], fnw[:], ALU.mult, ALU.mult, [xt[q], ssq, fnw], [xt[q]])
                        ST(y_out[t * 128:(t + 1) * 128, :], xt[q][:], [xt[q]], [y_out])
                if l == 0 and shard:
                    if os.environ.get('NO_CC'):
                        for r_ in range(8):
                            ST(hout[r_ * 3:(r_ + 1) * 3, :], hin[:], [hin], [hout])
                    else:
                        em.coll(ccs, [hin], [hout],
                                lambda e: e.collective_compute("AllGather", ALU.bypass, replica_groups=[list(range(8))],
                                                               ins=[hin.t.ap().opt()], outs=[hout.t.ap().opt()]))
                em.end_phase(em.scope_bufs.get(id(bes), []))
        em.barrier()
        print("instructions", em.n_ins, "waits", em.n_wait)
    return nc


_CACHE = {}


def kernel(**inp):
    f = lambda a: np.ascontiguousarray(np.asarray(a, dtype=np.float32))
    xp = f(inp['x_prompt'])[0]
    xsm = f(inp['x_sample'])
    NSH = 8 if SHARD else 1
    PTOK = xp.shape[0] // NSH
    if PTOK not in _CACHE:
        _CACHE[PTOK] = build(PTOK, SHARD)
    nc = _CACHE[PTOK]
    consts = host_consts()
    convw = np.concatenate([f(inp['gdn_conv_w']), f(inp['ssm_conv_w'])], axis=2)
    convb = np.concatenate([np.zeros((2, 768), np.float32), f(inp['ssm_conv_b'])], axis=1)
    gnw = np.tile(f(inp['gdn_norm_w']), (1, 4))
    rvec = np.stack([f(inp['rwkv_w0']), f(inp['rwkv_a0']), f(inp['rwkv_k_k']), f(inp['rwkv_k_a']),
                     f(inp['rwkv_r_k']).reshape(2, 256), f(inp['rwkv_ln_w']), f(inp['rwkv_ln_b'])], axis=1)
    common = {
        'consts': consts, 'norm1_w': f(inp['norm1_w']).reshape(2, 8, 128).transpose(0, 2, 1), 'w_in': f(inp['w_in']), 'convw': convw, 'convb': convb,
        'gdn_A_log': f(inp['gdn_A_log']), 'gdn_dt_bias': f(inp['gdn_dt_bias']), 'gdn_norm_w': gnw,
        'ssm_A_log': f(inp['ssm_A_log']), 'ssm_dt_bias': f(inp['ssm_dt_bias']), 'ssm_D': f(inp['ssm_D']),
        'ssm_norm_w': f(inp['ssm_norm_w']), 'rwkv_mu': f(inp['rwkv_mu']), 'rvec': rvec,
        'rwkv_w_up': f(inp['rwkv_w_up']), 'rwkv_a_up': f(inp['rwkv_a_up']), 'rwkv_g_up': f(inp['rwkv_g_up']),
        'w_out': f(inp['w_out']), 'norm2_w': f(inp['norm2_w']).reshape(2, 8, 128).transpose(0, 2, 1), 'ffn_w_gate': f(inp['ffn_w_gate']),
        'ffn_w_up': f(inp['ffn_w_up']), 'ffn_w_down': f(inp['ffn_w_down']),
        'final_norm_w': f(inp['final_norm_w']).reshape(1, D),
    }
    in_maps = []
    for c in range(8):
        sl = slice(c * NSEQ, (c + 1) * NSEQ)
        m = dict(common)
        cp = c if SHARD else 0
        m['xin'] = np.concatenate([xp[cp * PTOK:(cp + 1) * PTOK], xsm[sl].reshape(NSEQ * 64, D)], axis=0)
        xh = np.zeros((128, D), np.float32)
        if cp > 0:
            xh[0:3] = xp[c * PTOK - 3:c * PTOK]
        m['xhalo'] = xh
        oh = np.zeros((128, 8), np.float32)
        oh[:, c] = 1.0
        m['oh_me'] = oh
        sp = np.zeros((24, 128), np.float32)
        if cp > 0:
            for i_ in range(3):
                sp[3 * (c - 1) + i_, i_] = 1.0
        m['selprev'] = sp
        m['st_gdn'] = f(inp['state_gdn'])[:, sl]
        m['st_gdn_conv'] = f(inp['state_gdn_conv'])[:, sl]
        m['st_ssm'] = f(inp['state_ssm'])[:, sl]
        m['st_ssm_conv'] = f(inp['state_ssm_conv'])[:, sl]
        m['st_rwkv'] = f(inp['state_rwkv'])[:, sl]
        m['st_rwkv_shift'] = f(inp['state_rwkv_shift'])[:, sl]
        m = {k: np.ascontiguousarray(v) for k, v in m.items()}
        in_maps.append(m)
    res = run_bass_kernel_spmd(nc, in_maps, core_ids=list(range(8)))
    R = res.results
    y_prompt = np.concatenate([np.asarray(R[c]['y'])[0:PTOK] for c in range(NSH)], axis=0)[None]
    y_sample = np.concatenate([np.asarray(R[c]['y'])[PTOK:].reshape(NSEQ, 64, D) for c in range(8)], axis=0)
    cat = lambda k: np.concatenate([np.asarray(R[c][k]) for c in range(8)], axis=1)
    p1 = lambda k: np.asarray(R[NSH - 1][k])[:, None]
    outs = (y_prompt, y_sample,
            p1('o_pgdn'), p1('o_pgdn_conv'), p1('o_pssm'), p1('o_pssm_conv'), p1('o_prwkv'), p1('o_prwkv_shift'),
            cat('o_sgdn'), cat('o_sgdn_conv'), cat('o_sssm'), cat('o_sssm_conv'), cat('o_srwkv'), cat('o_srwkv_shift'))
    return tuple(np.ascontiguousarray(o, dtype=np.float32) for o in outs)
```

```python
import numpy as np
import concourse.bass as bass
import concourse.mybir as mybir
from concourse.bass_utils import run_bass_kernel_spmd
from contextlib import ExitStack

F32 = mybir.dt.float32
BF16 = mybir.dt.bfloat16
AF = mybir.ActivationFunctionType
ALU = mybir.AluOpType
AX = mybir.AxisListType

D = 1024
INC = 3600
FH = 2816
NJ = 22
GQ, GK, GV, GZ, GA, GB_ = 0, 256, 512, 768, 1024, 1028
SZ, SX, SBc, SCc, SDT = 1032, 1544, 2056, 2312, 2568
RR = 2576
NSEQ = 4
import os
USE_RR = bool(os.environ.get('USE_RR'))
SHARD = bool(int(os.environ.get('KSHARD', '1')))
NDS = int(os.environ.get("NDS", "4"))
EPS = 1e-6
GN_EPS = 64e-5


class Sem:
    def __init__(self, h, name):
        self.h = h
        self.name = name
        self.count = 0


class Buf:
    def __init__(self, t, name):
        self.t = t
        self.name = name
        self.lw = None
        self.rd = []

    def __getitem__(self, k):
        return self.t[k]


class Em:
    def __init__(self, nc, es):
        self.nc = nc
        self.es = es
        self.eng = {'pe': nc.tensor, 'act': nc.scalar, 'dve': nc.vector, 'pool': nc.gpsimd, 'sp': nc.sync}
        self.sems = []
        self.esem = {k: self.newsem('e_' + k) for k in ['pe', 'act', 'dve', 'pool']}
        self.known = {k: {} for k in self.eng}
        self.n_ins = 0
        self.n_wait = 0

    def newsem(self, name):
        s = Sem(self.es.enter_context(self.nc.semaphore(name)), name)
        self.sems.append(s)
        return s

    def sb(self, name, shape, dt=F32, es=None):
        self.n_sb = getattr(self, 'n_sb', 0) + 1
        name = "%s_%d" % (name, self.n_sb)
        if es is not None:
            if not hasattr(self, 'scope_bufs'):
                self.scope_bufs = {}
            b = Buf(es.enter_context(self.nc.sbuf_tensor(name, shape, dt)), name)
            self.scope_bufs.setdefault(id(es), []).append(b)
            return b
        return Buf((es or self.es).enter_context(self.nc.sbuf_tensor(name, shape, dt)), name)

    def ps(self, name, shape, dt=F32):
        return Buf(self.es.enter_context(self.nc.psum_tensor(name, shape, dt)), name)

    def dram(self, name, shape, dt=F32, kind=None):
        if kind is None:
            t = self.nc.dram_tensor(name, shape, dt)
        else:
            t = self.nc.dram_tensor(name, shape, dt, kind=kind)
        return Buf(t, name)

    def _waits(self, e, reads, writes):
        deps = {}

        def add(ent, raw):
            if ent is None:
                return
            s, v, en = ent
            if en == e and (e == 'pe' or not raw):
                return
            if deps.get(s, 0) < v:
                deps[s] = v
        for b in reads:
            add(b.lw, True)
        for b in writes:
            add(b.lw, False)
            for r in b.rd:
                add(r, False)
        kn = self.known[e]
        for s, v in deps.items():
            if kn.get(s, 0) >= v:
                continue
            self.eng[e].wait_ge(s.h, v)
            kn[s] = v
            self.n_wait += 1

    def op(self, e, fn, reads=(), writes=()):
        self._waits(e, reads, writes)
        ins = fn(self.eng[e])
        s = self.esem[e]
        s.count += 1
        ins.then_inc(s.h, 1)
        ent = (s, s.count, e)
        for b in reads:
            b.rd.append(ent)
            if len(b.rd) > 24:
                b.rd = b.rd[-24:] if False else b.rd
        for b in writes:
            b.lw = ent
            b.rd = []
        self.n_ins += 1
        return ins

    def buf_sem(self, b):
        if getattr(b, 'dsem', None) is None:
            if not hasattr(self, 'dpool'):
                self.dpool = []
                self.dnext = 0
            if len(self.dpool) < NDS:
                self.dpool.append(self.newsem("d%d" % len(self.dpool)))
            b.dsem = self.dpool[self.dnext % NDS]
            self.dnext += 1
        return b.dsem

    def end_phase(self, bufs):
        self.barrier()

    def coll(self, sem, reads, writes, fn):
        self._waits('pool', reads, writes)
        ins = fn(self.eng['pool'])
        sem.count += 1
        ins.then_inc(sem.h, 1)
        ent = (sem, sem.count, 'dma')
        for b in reads:
            b.rd.append(ent)
        for b in writes:
            b.lw = ent
            b.rd = []
        self.n_ins += 1
        self.eng['pool'].wait_ge(sem.h, sem.count)
        self.known['pool'][sem] = sem.count
        return ins

    def dma(self, q, sem, out_ap, in_ap, reads=(), writes=()):
        self._waits(q, reads, writes)
        sem = self.buf_sem(writes[0])
        ins = self.eng[q].dma_start(out=out_ap, in_=in_ap)
        sem.count += 16
        ins.then_inc(sem.h, 16)
        ent = (sem, sem.count, 'dma')
        for b in reads:
            b.rd.append(ent)
        for b in writes:
            b.lw = ent
            b.rd = []
        self.n_ins += 1
        return ins

    def rotate(self):
        self.barrier()
        for k in list(self.esem):
            self.n_rot = getattr(self, 'n_rot', 0) + 1
            if not hasattr(self, 'rot_pool'):
                self.rot_pool = {}
            pool = self.rot_pool.setdefault(k, [])
            if len(pool) < 2:
                pool.append(self.esem[k])
                if len(pool) < 2:
                    self.esem[k] = self.newsem('e2_' + k)
                    continue
            cur = self.esem[k]
            other = pool[0] if pool[1] is cur else pool[1]
            if cur not in pool:
                pool.append(cur)
            self.esem[k] = other

    def barrier(self):
        for e in self.eng:
            kn = self.known[e]
            for s in self.sems:
                if s.count > 0 and kn.get(s, 0) < s.count:
                    self.eng[e].wait_ge(s.h, s.count)
                    kn[s] = s.count
                    self.n_wait += 1


def host_consts():
    p = np.arange(128)[:, None]
    i = np.arange(128)[None, :]
    same = (p // 64) == (i // 64)
    c = {}
    c['tri_incl'] = (same & (p <= i)).astype(np.float32)
    c['tri_le'] = (p <= i).astype(np.float32)
    c['sups'] = (p > i).astype(np.float32)
    c['mstrict'] = (same & (i > p)).astype(np.float32)
    c['blk'] = same.astype(np.float32)
    c['ident'] = np.eye(128, dtype=np.float32)
    rowc = np.zeros((128, 2, 128), np.float32)
    rowc[0:64, 0, :] = 1
    rowc[64:128, 1, :] = 1
    c['rowc'] = rowc.reshape(128, 256)
    ch = np.zeros((128, 2), np.float32)
    ch[0:64, 0] = 1
    ch[64:, 1] = 1
    c['chsel'] = ch
    return np.concatenate([c[k] for k in ['tri_incl', 'tri_le', 'sups', 'mstrict', 'blk', 'ident', 'rowc', 'chsel']], axis=1)


NCONST = 128 * 6 + 256 + 2


def build(PTOK, shard=SHARD):
    assert PTOK % 128 == 0
    PT = PTOK // 128
    NT = PT + 2
    NTOK = NT * 128
    SROWS = 3 + PTOK + NSEQ * 67
    nc = bass.Bass("TRN2", target_bir_lowering=False)
    es = ExitStack()
    with es:
        em = Em(nc, es)
        DI = lambda n, s: em.dram(n, s, F32, kind="ExternalInput")
        DO = lambda n, s: em.dram(n, s, F32, kind="ExternalOutput")
        xin = DI("xin", [NTOK, D])
        xhalo = DI("xhalo", [128, D])
        oh_me = DI("oh_me", [128, 8])
        selprev = DI("selprev", [24, 128])
        consts_d = DI("consts", [128, NCONST])
        st_gdn = DI("st_gdn", [2, NSEQ, 4, 64, 64])
        st_gdn_conv = DI("st_gdn_conv", [2, NSEQ, 3, 768])
        st_ssm = DI("st_ssm", [2, NSEQ, 8, 64, 128])
        st_ssm_conv = DI("st_ssm_conv", [2, NSEQ, 3, 1024])
        st_rwkv = DI("st_rwkv", [2, NSEQ, 4, 64, 64])
        st_rwkv_shift = DI("st_rwkv_shift", [2, NSEQ, 1, 1024])
        norm1_w = DI("norm1_w", [2, 128, 8]); w_in = DI("w_in", [2, D, INC])
        convw = DI("convw", [2, 4, 1792]); convb = DI("convb", [2, 1792])
        gdn_A_log = DI("gdn_A_log", [2, 4]); gdn_dt_bias = DI("gdn_dt_bias", [2, 4]); gdn_norm_w = DI("gdn_norm_w", [2, 256])
        ssm_A_log = DI("ssm_A_log", [2, 8]); ssm_dt_bias = DI("ssm_dt_bias", [2, 8]); ssm_D = DI("ssm_D", [2, 8])
        ssm_norm_w = DI("ssm_norm_w", [2, 512])
        rwkv_mu = DI("rwkv_mu", [2, 1024])
        rvec = DI("rvec", [2, 7, 256])
        rwkv_w_up = DI("rwkv_w_up", [2, 64, 256]); rwkv_a_up = DI("rwkv_a_up", [2, 64, 256]); rwkv_g_up = DI("rwkv_g_up", [2, 128, 256])
        w_out = DI("w_out", [2, D, D]); norm2_w = DI("norm2_w", [2, 128, 8])
        w_gate = DI("ffn_w_gate", [2, D, FH]); w_up = DI("ffn_w_up", [2, D, FH]); w_down = DI("ffn_w_down", [2, FH, D])
        final_norm_w = DI("final_norm_w", [1, D])

        y_out = DO("y", [NTOK, D])
        o_pgdn = DO("o_pgdn", [2, 4, 64, 64]); o_pgdn_conv = DO("o_pgdn_conv", [2, 3, 768])
        o_pssm = DO("o_pssm", [2, 8, 64, 128]); o_pssm_conv = DO("o_pssm_conv", [2, 3, 1024])
        o_prwkv = DO("o_prwkv", [2, 4, 64, 64]); o_prwkv_shift = DO("o_prwkv_shift", [2, 1, 1024])
        o_sgdn = DO("o_sgdn", [2, NSEQ, 4, 64, 64]); o_sgdn_conv = DO("o_sgdn_conv", [2, NSEQ, 3, 768])
        o_sssm = DO("o_sssm", [2, NSEQ, 8, 64, 128]); o_sssm_conv = DO("o_sssm_conv", [2, NSEQ, 3, 1024])
        o_srwkv = DO("o_srwkv", [2, NSEQ, 4, 64, 64]); o_srwkv_shift = DO("o_srwkv_shift", [2, NSEQ, 1, 1024])
        outs_all = [y_out, o_pgdn, o_pgdn_conv, o_pssm, o_pssm_conv, o_prwkv, o_prwkv_shift,
                    o_sgdn, o_sgdn_conv, o_sssm, o_sssm_conv, o_srwkv, o_srwkv_shift]

        win_b = em.dram("win_b", [2, D, INC], BF16)
        wout_b = em.dram("wout_b", [2, D, D], BF16)
        wg_b = em.dram("wg_b", [2, D, FH], BF16)
        wu_b = em.dram("wu_b", [2, D, FH], BF16)
        wd_b = em.dram("wd_b", [2, FH, D], BF16)
        projs = em.dram("projs", [SROWS, INC], F32)
        mixTs = em.dram("mixTs", [NT, 128, 8, 128], BF16)
        x1s = em.dram("x1s", [NTOK, D], F32)
        AGW = 1544
        agin_b = em.dram("agin", [16, 8 * AGW], F32)
        agout_b = em.dram("agout", [128, 8 * AGW], F32)

        class _V:
            def __init__(self, b, ap):
                self.b = b
                self.ap_ = ap

            def __getitem__(self, k):
                return self.ap_[k]
        agin_v = agin_b.t.ap().rearrange("i (j c) -> (i j) c", j=8)
        agout_v = agout_b.t.ap().rearrange("q (j c) -> (q j) c", j=8)
        hin = em.dram("hin", [3, D], F32)
        hout = em.dram("hout", [24, D], F32)
        ccs = em.newsem("ccs")
        ccs_l = [em.newsem("ccs_l%d" % i) for i in range(2)]
        gSin = em.sb("gSin", [64, 256]); rSin = em.sb("rSin", [64, 256]); sSin = em.sb("sSin", [128, 512])

        PS = [em.ps("ps%d" % i, [128, 512]) for i in range(8)]
        psi = [0]

        def P():
            b = PS[psi[0] % 8]
            psi[0] += 1
            return b

        ld = [None] * 4
        ldi = [0]

        def LD(out_ap, in_ap, r, w):
            s = ld[ldi[0] % 4]
            ldi[0] += 1
            em.dma('sp', s, out_ap, in_ap, reads=r, writes=w)

        stq = None

        def ST(out_ap, in_ap, r, w):
            em.dma('sp', stq, out_ap, in_ap, reads=r, writes=w)

        def mm(out_ap, lhsT, rhs, r, w, start=True, stop=True):
            em.op('pe', lambda e: e.matmul(out_ap, lhsT=lhsT, rhs=rhs, start=start, stop=stop), r, w)

        def act(out_ap, in_ap, func, r, w, bias=None, scale=None, accum=None):
            kw = {}
            if bias is not None:
                kw['bias'] = bias
            if scale is not None:
                kw['scale'] = scale
            if accum is not None:
                kw['accum_out'] = accum
            em.op('act', lambda e: e.activation(out=out_ap, in_=in_ap, func=func, **kw), r, w)

        def tt(eng, out_ap, in0, in1, op, r, w):
            em.op(eng, lambda e: e.tensor_tensor(out=out_ap, in0=in0, in1=in1, op=op), r, w)

        def tsc(eng, out_ap, in0, s1, op0, r, w, s2=None, op1=None):
            if op1 is None:
                em.op(eng, lambda e: e.tensor_scalar(out=out_ap, in0=in0, scalar1=s1, scalar2=None, op0=op0), r, w)
            else:
                em.op(eng, lambda e: e.tensor_scalar(out=out_ap, in0=in0, scalar1=s1, scalar2=s2, op0=op0, op1=op1), r, w)

        def stt(eng, out_ap, in0, scalar, in1, op0, op1, r, w):
            em.op(eng, lambda e: e.scalar_tensor_tensor(out=out_ap, in0=in0, scalar=scalar, in1=in1, op0=op0, op1=op1), r, w)

        def red(out_ap, in_ap, r, w):
            em.op('dve', lambda e: e.tensor_reduce(out=out_ap, in_=in_ap, axis=AX.X, op=ALU.add), r, w)

        def sigmoid(out_ap, in_ap, r, w, scale=1.0):
            act(out_ap, in_ap, AF.Exp, r, w, scale=-scale)
            act(out_ap, out_ap, AF.Ln, w, w, bias=1.0)
            act(out_ap, out_ap, AF.Exp, w, w, scale=-1.0)

        def rsqrt(out_ap, in_ap, r, w, eps_ap, scale=1.0):
            act(out_ap, in_ap, AF.Ln, r + [epsb], w, bias=eps_ap, scale=scale)
            act(out_ap, out_ap, AF.Exp, w, w, scale=-0.5)

        cst = em.sb("cst", [128, NCONST])
        LD(cst[:], consts_d[:], [consts_d], [cst])
        TRI = cst[:, 0:128]; TLE = cst[:, 128:256]; SUPS = cst[:, 256:384]; MSTR = cst[:, 384:512]
        BLK = cst[:, 512:640]; IDF = cst[:, 640:768]
        ROWC = lambda c: cst[:, 768 + c * 128: 768 + (c + 1) * 128]
        CHSEL = cst[:, 1024:1026]
        identb = em.sb("identb", [128, 128], BF16)
        act(identb[:], IDF, AF.Copy, [cst], [identb])
        epsb = em.sb("epsb", [128, 2])
        em.op('pool', lambda e: e.memset(epsb[:, 0:1], EPS), [], [epsb])
        em.op('pool', lambda e: e.memset(epsb[:, 1:2], GN_EPS), [epsb], [epsb])
        zero_t = em.sb("zero_t", [3, INC])
        em.op('pool', lambda e: e.memset(zero_t[:], 0.0), [], [zero_t])

        with ExitStack() as wes:
            stg = [em.sb("stg%d" % i, [128, INC], F32, wes) for i in range(3)]
            stb = [em.sb("stb%d" % i, [128, INC], BF16, wes) for i in range(2)]
            em.buf_sem(cst)
            while len(em.dpool) < NDS:
                em.dpool.append(em.newsem("d%d" % len(em.dpool)))
            for i_ in range(3):
                stg[i_].dsem = em.dpool[i_]
            for b_ in (win_b, wout_b, wg_b, wu_b, wd_b):
                b_.dsem = em.dpool[3]
            jobs = []
            for l in range(2):
                for (src, dst, rows, cols) in [(w_in, win_b, D, INC), (w_out, wout_b, D, D), (w_gate, wg_b, D, FH),
                                               (w_up, wu_b, D, FH), (w_down, wd_b, FH, D)]:
                    for r0 in range(0, rows, 128):
                        jobs.append((src, dst, l, r0, cols))

            def wload(k):
                src, dst, l_, r0, cols = jobs[k]
                a = stg[k % 3]
                LD(a[:, 0:cols], src[l_, r0:r0 + 128, :], [src], [a])
            wload(0)
            wload(1)
            for k, (src, dst, l_, r0, cols) in enumerate(jobs):
                if k + 2 < len(jobs):
                    wload(k + 2)
                a, b = stg[k % 3], stb[k % 2]
                if k % 3 == 0:
                    act(b[:, 0:cols], a[:, 0:cols], AF.Copy, [a], [b])
                elif k % 3 == 1:
                    em.op('dve', lambda e, a=a, b=b, cols=cols: e.tensor_copy(out=b[:, 0:cols], in_=a[:, 0:cols]), [a], [b])
                else:
                    em.op('pool', lambda e, a=a, b=b, cols=cols: e.tensor_copy(out=b[:, 0:cols], in_=a[:, 0:cols]), [a], [b])
                ST(dst[l_, r0:r0 + 128, :], b[:, 0:cols], [b], [dst])
            em.end_phase(em.scope_bufs.get(id(wes), []))

        def seq_rows(t, c):
            if t < PT:
                return 3 + t * 128 + c * 64
            s = 2 * (t - PT) + c
            return 3 + PTOK + s * 67 + 3

        for l in range(2):
            xsrc = xin if l == 0 else x1s
            for s in range(NSEQ):
                b0 = 3 + PTOK + s * 67
                ST(projs[b0:b0 + 3, 0:768], st_gdn_conv[l, s], [st_gdn_conv], [projs])
                ST(projs[b0:b0 + 3, SX:SX + 1024], st_ssm_conv[l, s], [st_ssm_conv], [projs])
                ST(projs[b0 + 2:b0 + 3, RR:RR + 1024], st_rwkv_shift[l, s], [st_rwkv_shift], [projs])

            with ExitStack() as pes:
                winr = em.sb("winr", [128, 8, INC], BF16, pes)
                for kk in range(8):
                    LD(winr[:, kk, :], win_b[l, kk * 128:(kk + 1) * 128, :], [win_b], [winr])
                n1w = em.sb("n1w", [128, 8], F32, pes)
                LD(n1w[:], norm1_w[l], [norm1_w], [n1w])
                xt = [em.sb("xt%d" % i, [128, D], F32, pes) for i in range(2)]
                junk = em.sb("junk", [128, D], F32, pes)
                ssq = [em.sb("ssq%d" % i, [128, 2], F32, pes) for i in range(2)]
                hnb = [em.sb("hnb%d" % i, [128, D], BF16, pes) for i in range(2)]
                hnT = [em.sb("hnT%d" % i, [128, 8, 128], BF16, pes) for i in range(2)]
                pj = [em.sb("pj%d" % i, [128, INC], F32, pes) for i in range(2)]
                if l == 1 and shard:
                    hall = em.sb("hall", [24, D], F32, pes)
                    selp = em.sb("selp", [24, 128], F32, pes)
                    LD(hall[:], hout[:], [hout], [hall])
                    LD(selp[:], selprev[:], [selprev], [selp])
                for t in range(-1, NT):
                    q = t % 2
                    if t >= 0:
                        LD(xt[q][:], xsrc[t * 128:(t + 1) * 128, :], [xsrc], [xt[q]])
                    elif l == 0 or not shard:
                        LD(xt[q][:], xhalo[:], [xhalo], [xt[q]])
                    else:
                        for cb_ in range(2):
                            pb = P()
                            mm(pb[:], selp[:], hall[:, cb_ * 512:(cb_ + 1) * 512], [selp, hall], [pb])
                            act(xt[q][:, cb_ * 512:(cb_ + 1) * 512], pb[:], AF.Copy, [pb], [xt[q]])
                    act(junk[:], xt[q][:], AF.Square, [xt[q]], [junk, ssq[q]], accum=ssq[q][:, 0:1])
                    rsqrt(ssq[q][:, 1:2], ssq[q][:, 0:1], [ssq[q]], [ssq[q]], epsb[:, 0:1], scale=1.0 / D)
                    act(hnb[q][:], xt[q][:], AF.Copy, [xt[q], ssq[q]], [hnb[q]], scale=ssq[q][:, 1:2])
                    for half in range(2):
                        pb = P()
                        for kk in range(4):
                            kc = half * 4 + kk
                            mm(pb[:, kk * 128:(kk + 1) * 128], hnb[q][:, kc * 128:(kc + 1) * 128], identb[:],
                               [hnb[q], identb], [pb])
                        for kk in range(4):
                            kc = half * 4 + kk
                            tsc('dve', hnT[q][:, kc, :], pb[:, kk * 128:(kk + 1) * 128], n1w[:, kc:kc + 1], ALU.mult,
                                [pb, n1w], [hnT[q]])
                    for c0 in range(0, INC, 512):
                        cw_ = min(512, INC - c0)
                        pb = P()
                        for kc in range(8):
                            mm(pb[:, 0:cw_], hnT[q][:, kc, :], winr[:, kc, c0:c0 + cw_], [hnT[q], winr], [pb],
                               start=(kc == 0), stop=(kc == 7))
                        act(pj[q][:, c0:c0 + cw_], pb[:, 0:cw_], AF.Copy, [pb], [pj[q]])
                    if t < 0:
                        ST(projs[0:3, :], pj[q][0:3, :], [pj[q]], [projs])
                        continue
                    for c in range(2):
                        r0 = seq_rows(t, c)
                        if t < PT and c == 1:
                            continue
                        nr = 128 if t < PT else 64
                        ST(projs[r0:r0 + nr, :], pj[q][c * 64:c * 64 + nr, :], [pj[q]], [projs])
                em.end_phase(em.scope_bufs.get(id(pes), []))

            e0 = 3 + PTOK
            ST(o_pgdn_conv[l], projs[e0 - 3:e0, 0:768], [projs], [o_pgdn_conv])
            ST(o_pssm_conv[l], projs[e0 - 3:e0, SX:SX + 1024], [projs], [o_pssm_conv])
            ST(o_prwkv_shift[l], projs[e0 - 1:e0, RR:RR + 1024], [projs], [o_prwkv_shift])
            for s in range(NSEQ):
                e1 = 3 + PTOK + s * 67 + 67
                ST(o_sgdn_conv[l, s], projs[e1 - 3:e1, 0:768], [projs], [o_sgdn_conv])
                ST(o_sssm_conv[l, s], projs[e1 - 3:e1, SX:SX + 1024], [projs], [o_sssm_conv])
                ST(o_srwkv_shift[l, s], projs[e1 - 1:e1, RR:RR + 1024], [projs], [o_srwkv_shift])

            def phase_M(mode):
                SO = (mode == "state")
                W = 128 if SO else 64
                with ExitStack() as mes:
                    def bc(name, src_ap, n, dt=F32):
                        b = em.sb(name, [128, n], dt, mes)
                        LD(b[:], src_ap.partition_broadcast(128), [], [b])
                        return b
                    cwt = em.sb("cwt", [128, 4, 1792], F32, mes)
                    for j in range(4):
                        LD(cwt[:, j, :], convw[l, j:j + 1, :].partition_broadcast(128), [], [cwt])
                    cbt = bc("cbt", convb[l:l + 1, :], 1792)
                    mut = bc("mut", rwkv_mu[l:l + 1, :], 1024)
                    rv = em.sb("rv", [128, 7, 256], F32, mes)
                    for j in range(7):
                        LD(rv[:, j, :], rvec[l, j:j + 1, :].partition_broadcast(128), [], [rv])
                    W0, A0, KK_, KA_, RK_, LNW, LNB = [rv[:, j, :] for j in range(7)]
                    snw = bc("snw", ssm_norm_w[l:l + 1, :], 512)
                    gnw = bc("gnw", gdn_norm_w[l:l + 1, :], 256)
                    sm = em.sb("smallc", [128, 64], F32, mes)
                    LD(sm[:, 0:4], gdn_A_log[l:l + 1, :].partition_broadcast(128), [], [sm])
                    LD(sm[:, 4:8], gdn_dt_bias[l:l + 1, :].partition_broadcast(128), [], [sm])
                    LD(sm[:, 8:16], ssm_A_log[l:l + 1, :].partition_broadcast(128), [], [sm])
                    LD(sm[:, 16:24], ssm_dt_bias[l:l + 1, :].partition_broadcast(128), [], [sm])
                    LD(sm[:, 24:32], ssm_D[l:l + 1, :].partition_broadcast(128), [], [sm])
                    act(sm[:, 32:36], sm[:, 0:4], AF.Exp, [sm], [sm])
                    act(sm[:, 40:48], sm[:, 8:16], AF.Exp, [sm], [sm])
                    GAe, GDTB, SAe, SDTB, SDD = sm[:, 32:36], sm[:, 4:8], sm[:, 40:48], sm[:, 16:24], sm[:, 24:32]
                    lf = em.sb("loraf", [128, 3, 256], F32, mes)
                    LD(lf[0:64, 0, :], rwkv_w_up[l], [], [lf])
                    LD(lf[0:64, 1, :], rwkv_a_up[l], [], [lf])
                    LD(lf[:, 2, :], rwkv_g_up[l], [], [lf])
                    lb = em.sb("lorab", [128, 3, 256], BF16, mes)
                    act(lb[0:64, 0:2, :], lf[0:64, 0:2, :], AF.Copy, [lf], [lb])
                    act(lb[:, 2, :], lf[:, 2, :], AF.Copy, [lf], [lb])

                    gS = em.sb("gS", [64, 4 * W], F32, mes); gSb = em.sb("gSb", [64, 4 * W], BF16, mes)
                    sS = em.sb("sS", [128, 512], F32, mes); sSb = em.sb("sSb", [128, 512], BF16, mes)
                    rH = em.sb("rH", [64, 4 * W], F32, mes); rHb = em.sb("rHb", [64, 4 * W], BF16, mes)
                    if SO:
                        for b_ in (gS, sS, rH):
                            em.op('pool', lambda e, b_=b_: e.memset(b_[:], 0.0), [], [b_])
                        for b_ in (gS, rH):
                            for h in range(4):
                                em.op('pool', lambda e, b_=b_, h=h: e.tensor_copy(out=b_[:, h * W + 64:(h + 1) * W], in_=IDF[0:64, 0:64]),
                                      [cst, b_], [b_])
                        s_tot = em.sb("s_tot", [128, 8], F32, mes)
                        em.op('pool', lambda e: e.memset(s_tot[:], 0.0), [], [s_tot])
                    else:
                        for (b_, src_) in ((gS, gSin), (sS, sSin), (rH, rSin)):
                            em.op('pool', lambda e, b_=b_, src_=src_: e.tensor_copy(out=b_[:], in_=src_[:]), [src_], [b_])
                    for (f_, b_) in ((gS, gSb), (sS, sSb), (rH, rHb)):
                        em.op('pool', lambda e, f_=f_, b_=b_: e.tensor_copy(out=b_[:], in_=f_[:]), [f_], [b_])
                    stio = em.sb("stio", [64, 1024], F32, mes)

                    xs = [em.sb("xs%d" % d_, [128, 1792], F32, mes) for d_ in range(4)]
                    pz = em.sb("pz", [128, 776], F32, mes)
                    pr = em.sb("pr", [128, 1032], F32, mes)
                    prv = em.sb("prv", [128, 1024], F32, mes)
                    if not SO:
                        mix = em.sb("mix", [128, D], F32, mes)
                        mixb = em.sb("mixb", [128, D], BF16, mes)
                        mixT = em.sb("mixT", [128, 8, 128], BF16, mes)

                    cnt = [0]

                    def T(shape_cols, dt=F32, rows=128):
                        cnt[0] += 1
                        return em.sb("t%d_%d" % (l, cnt[0]), [rows, shape_cols], dt, mes)

                    g_sq = T(512); g_ss = T(16); g_qk = T(512); g_qkb = T(512, BF16)
                    g_gt = T(32)
                    g_bd = T(8, F32, 64)
                    g_kb = T(256); g_kbb = T(256, BF16); g_qdb = T(256, BF16); g_kd = [T(256, BF16) for _ in range(2)]
                    g_kbg = T(256, BF16); g_vb = T(256, BF16)
                    g_fT = T(16 * 128, BF16, 64)
                    g_Gs = T(512); g_Dm = T(512); g_DmS = T(512)
                    g_attnT = T(512, BF16); g_LT = T(512, BF16); g_L = T(512, BF16)
                    nA = [T(512, BF16) for _ in range(2)]; nAT = [T(512, BF16) for _ in range(2)]; nP = [T(512, BF16) for _ in range(2)]
                    g_nwT = T(512, BF16, 64)
                    g_vn = T(4 * W, BF16); g_o = T(256)
                    s_t = T(64)
                    s_bd = T(16)
                    s_xdt = T(512, BF16); s_xde = [T(512, BF16) for _ in range(2)]; s_xD = g_Dm
                    s_BCb = T(512, BF16); s_BCT = T(512, BF16)
                    big1 = T(1024); s_Gs = big1; s_E = prv; s_BCm = T(256); s_scT = T(1024, BF16)
                    s_y = g_DmS; s_tmp = g_Gs
                    s_ss = T(8)
                    r_xm = big1; r_lin = T(256, BF16); r_linT = T(256, BF16)
                    r_lw = T(256); r_icl = T(256); r_gate = T(256); r_tmp = T(256); r_tmp2 = T(256)
                    r_kk = T(256); r_k2 = T(256); r_ss = T(16)
                    r_cs = T(256); r_e = T(256)
                    r_at = T(256, BF16); r_rt = T(256, BF16); r_bt = T(256, BF16); r_kt = T(256, BF16)
                    r_bh = [T(256, BF16) for _ in range(2)]; r_kh = [T(256, BF16) for _ in range(2)]
                    r_vb = T(256, BF16)
                    r_fT = g_fT
                    r_AB = s_scT; r_AK = T(1024, BF16)
                    r_A0 = g_LT; r_A0T = g_L
                    r_gC = T(8, F32, 64)
                    r_X = T(4 * W, BF16); r_U = T(4 * W, BF16); r_y = T(256)

                    for b_ in (g_vn, r_X, r_U):
                        em.op('pool', lambda e, b_=b_: e.memset(b_[:], 0.0), [], [b_])

                    def neumann(A0, A0T, hooks=()):
                        hooks = list(hooks)
                        Pc = nP[0]
                        tt('pool', Pc[:].rearrange("p (h i) -> p h i", h=4), A0[:].rearrange("p (h i) -> p h i", h=4),
                           identb[:].unsqueeze(1).to_broadcast([128, 4, 128]), ALU.add, [A0, identb], [Pc])
                        Ac, ATc = A0, A0T
                        for lev in range(1, 6):
                            An, ATn = nA[lev % 2], nAT[lev % 2]
                            pb1 = P()
                            for h in range(4):
                                hs = slice(h * 128, (h + 1) * 128)
                                mm(pb1[:, hs], Ac[:, hs], ATc[:, hs], [Ac, ATc], [pb1])
                            act(ATn[:], pb1[:], AF.Copy, [pb1], [ATn])
                            if lev < 5:
                                pb2 = P()
                                for h in range(4):
                                    hs = slice(h * 128, (h + 1) * 128)
                                    mm(pb2[:, hs], ATc[:, hs], Ac[:, hs], [Ac, ATc], [pb2])
                                em.op('dve', lambda e, An=An, pb2=pb2: e.tensor_copy(out=An[:], in_=pb2[:]), [pb2], [An])
                            pb3 = P()
                            for h in range(4):
                                hs = slice(h * 128, (h + 1) * 128)
                                mm(pb3[:, hs], ATn[:, hs], Pc[:, hs], [ATn, Pc], [pb3])
                            Pn = nP[lev % 2]
                            tt('dve', Pn[:], pb3[:], Pc[:], ALU.add, [pb3, Pc], [Pn])
                            Pc, Ac, ATc = Pn, An, ATn
                            if hooks:
                                hooks.pop(0)()
                        while hooks:
                            hooks.pop(0)()
                        return Pc

                    def transpose_heads(dst, dst_idx, src, col0, srcbuf):
                        pb = P()
                        for h in range(4):
                            mm(pb[0:64, h * 128:(h + 1) * 128], src[:, col0 + h * 64: col0 + (h + 1) * 64], identb[:],
                               [srcbuf, identb], [pb])
                        return pb

                    for t in (range(PT) if SO else range(NT)):
                        is_s = t >= PT
                        for c in range(2):
                            if t < PT and c == 1:
                                continue
                            r0 = seq_rows(t, c)
                            nr = 128 if t < PT else 64
                            rs = slice(c * 64, c * 64 + nr)
                            for d_ in range(4):
                                LD(xs[d_][rs, 0:768], projs[r0 - d_:r0 - d_ + nr, 0:768], [projs], [xs[d_]])
                                LD(xs[d_][rs, 768:1792], projs[r0 - d_:r0 - d_ + nr, SX:SX + 1024], [projs], [xs[d_]])
                            LD(pz[rs, :], projs[r0:r0 + nr, 768:1544], [projs], [pz])
                            LD(pr[rs, :], projs[r0:r0 + nr, 2568:3600], [projs], [pr])

                        for d_ in (2, 0, 3, 1):
                            tt('pool' if d_ >= 2 else 'dve', xs[d_][:], xs[d_][:], cwt[:, 3 - d_, :], ALU.mult, [xs[d_], cwt], [xs[d_]])
                        tt('dve', xs[0][:], xs[0][:], xs[1][:], ALU.add, [xs[0], xs[1]], [xs[0]])
                        tt('dve', xs[2][:], xs[2][:], xs[3][:], ALU.add, [xs[2], xs[3]], [xs[2]])
                        tt('dve', xs[0][:], xs[0][:], xs[2][:], ALU.add, [xs[0], xs[2]], [xs[0]])
                        tt('dve', xs[0][:], xs[0][:], cbt[:], ALU.add, [xs[0], cbt], [xs[0]])
                        sigmoid(xs[1][:], xs[0][:], [xs[0]], [xs[1]])
                        tt('dve', xs[0][:], xs[0][:], xs[1][:], ALU.mult, [xs[0], xs[1]], [xs[0]])
                        cv = xs[0]

                        act(g_sq[:], cv[:, 0:512], AF.Square, [cv], [g_sq])
                        red(g_ss[:, 0:8], g_sq[:].rearrange("p (h d) -> p h d", h=8), [g_sq], [g_ss])
                        rsqrt(g_ss[:, 8:16], g_ss[:, 0:8], [g_ss], [g_ss], epsb[:, 0:1])
                        tsc('dve', g_ss[:, 8:12], g_ss[:, 8:12], 0.125, ALU.mult, [g_ss], [g_ss])
                        tt('dve', g_qk[:].rearrange("p (h d) -> p h d", h=8), cv[:, 0:512].rearrange("p (h d) -> p h d", h=8),
                           g_ss[:, 8:16].unsqueeze(2).to_broadcast([128, 8, 64]), ALU.mult, [cv, g_ss], [g_qk])
                        act(g_qkb[:], g_qk[:], AF.Copy, [g_qk], [g_qkb])
                        tt('dve', g_gt[:, 24:28], pz[:, GA - 768:GA - 768 + 4], GDTB, ALU.add, [pz, sm], [g_gt])
                        act(g_gt[:, 24:28], g_gt[:, 24:28], AF.Exp, [g_gt], [g_gt])
                        act(g_gt[:, 24:28], g_gt[:, 24:28], AF.Ln, [g_gt], [g_gt], bias=1.0)
                        stt('dve', g_gt[:, 0:4], g_gt[:, 24:28], -1.0, GAe, ALU.mult, ALU.mult, [g_gt, sm], [g_gt])
                        sigmoid(g_gt[:, 4:8], pz[:, GB_ - 768:GB_ - 768 + 4], [pz], [g_gt])
                        pb = P()
                        mm(pb[:, 0:4], TRI, g_gt[:, 0:4], [cst, g_gt], [pb])
                        mm(pb[:, 4:8], BLK, g_gt[:, 0:4], [cst, g_gt], [pb])
                        for c in range(2):
                            mm(pb[0:64, 8 + 4 * c:12 + 4 * c], ROWC(c)[:, 0:64], g_gt[:, 0:4], [cst, g_gt], [pb])
                        act(g_gt[:, 8:12], pb[:, 0:4], AF.Copy, [pb], [g_gt])
                        act(g_gt[:, 12:16], pb[:, 0:4], AF.Exp, [pb], [g_gt])
                        tt('dve', g_gt[:, 24:28], pb[:, 4:8], g_gt[:, 8:12], ALU.subtract, [pb, g_gt], [g_gt])
                        act(g_gt[:, 24:28], g_gt[:, 24:28], AF.Exp, [g_gt], [g_gt])
                        for c in range(2):
                            tsc('dve', g_gt[:, 16 + 4 * c:20 + 4 * c], g_gt[:, 24:28], CHSEL[:, c:c + 1], ALU.mult, [g_gt, cst], [g_gt])
                        act(g_bd[:], pb[0:64, 8:16], AF.Exp, [pb], [g_bd])
                        def hb(ap4):
                            return ap4.unsqueeze(2).to_broadcast([128, 4, 64])
                        def h3(ap):
                            return ap.rearrange("p (h d) -> p h d", h=4)
                        tt('dve', h3(g_kb[:]), h3(g_qk[:, 256:512]), hb(g_gt[:, 4:8]), ALU.mult, [g_qk, g_gt], [g_kb])
                        act(g_kbb[:], g_kb[:], AF.Copy, [g_kb], [g_kbb])
                        if not SO:
                            tt('pool', h3(g_qdb[:]), h3(g_qk[:, 0:256]), hb(g_gt[:, 12:16]), ALU.mult, [g_qk, g_gt], [g_qdb])
                        for c in range(2):
                            tt('dve', h3(g_kd[c][:]), h3(g_qk[:, 256:512]), hb(g_gt[:, 16 + 4 * c:20 + 4 * c]), ALU.mult,
                               [g_qk, g_gt], [g_kd[c]])
                        tt('pool', h3(g_kbg[:]), h3(g_kb[:]), hb(g_gt[:, 12:16]), ALU.mult, [g_kb, g_gt], [g_kbg])
                        tt('pool', h3(g_vb[:]), h3(cv[:, 512:768]), hb(g_gt[:, 4:8]), ALU.mult, [cv, g_gt], [g_vb])
                        fT = g_fT[:].rearrange("p (w i) -> p w i", w=16)
                        for w_, (src, col0) in enumerate([(g_qkb, 256), (g_qkb, 0), (g_kbb, 0), (g_qdb, 0)]):
                            if SO and w_ in (1, 3):
                                continue
                            pbt = transpose_heads(None, None, src, col0, src)
                            act(g_fT[:, w_ * 512:(w_ + 1) * 512], pbt[0:64, :], AF.Copy, [pbt], [g_fT])
                        kT = lambda h: fT[:, 0 + h, :]
                        qT = lambda h: fT[:, 4 + h, :]
                        kbT = lambda h: fT[:, 8 + h, :]
                        qdT = lambda h: fT[:, 12 + h, :]
                        tt('pool', g_Gs[:].rearrange("p (h j) -> p h j", h=4), SUPS.unsqueeze(1).to_broadcast([128, 4, 128]),
                           g_gt[:, 0:4].unsqueeze(2).to_broadcast([128, 4, 128]), ALU.mult, [cst, g_gt], [g_Gs])
                        pbD = P()
                        for h in range(4):
                            mm(pbD[:, h * 128:(h + 1) * 128], g_Gs[:, h * 128:(h + 1) * 128], TLE, [g_Gs, cst], [pbD])
                        act(g_Dm[:], pbD[:], AF.Exp, [pbD], [g_Dm])
                        tt('pool', g_DmS[:].rearrange("p (h i) -> p h i", h=4), g_Dm[:].rearrange("p (h i) -> p h i", h=4),
                           MSTR.unsqueeze(1).to_broadcast([128, 4, 128]), ALU.mult, [g_Dm, cst], [g_DmS])
                        pbK = P()
                        if not SO:
                            tt('pool', g_Dm[:].rearrange("p (h i) -> p h i", h=4), g_Dm[:].rearrange("p (h i) -> p h i", h=4),
                               TRI.unsqueeze(1).to_broadcast([128, 4, 128]), ALU.mult, [g_Dm, cst], [g_Dm])
                            pbQ = P()
                        for h in range(4):
                            hs = slice(h * 128, (h + 1) * 128)
                            if not SO:
                                mm(pbQ[:, hs], kT(h), qT(h), [g_fT], [pbQ])
                            mm(pbK[:, hs], kT(h), kbT(h), [g_fT], [pbK])
                        if not SO:
                            tt('dve', g_attnT[:], pbQ[:], g_Dm[:], ALU.mult, [pbQ, g_Dm], [g_attnT])
                        stt('dve', g_LT[:], pbK[:], -1.0, g_DmS[:], ALU.mult, ALU.mult, [pbK, g_DmS], [g_LT])
                        pbT = P()
                        for h in range(4):
                            hs = slice(h * 128, (h + 1) * 128)
                            mm(pbT[:, hs], g_LT[:, hs], identb[:], [g_LT, identb], [pbT])
                        act(g_L[:], pbT[:], AF.Copy, [pbT], [g_L])
                        XO, BO, CO = 768, 1280, 1536
                        def h8(ap):
                            return ap.rearrange("p (h d) -> p h d", h=8)
                        def hb8(ap8):
                            return ap8.unsqueeze(2).to_broadcast([128, 8, 64])
                        def ssd_p1():
                            tt('dve', s_t[:, 48:56], pr[:, 0:8], SDTB, ALU.add, [pr, sm], [s_t])
                            act(s_t[:, 48:56], s_t[:, 48:56], AF.Exp, [s_t], [s_t])
                            act(s_t[:, 0:8], s_t[:, 48:56], AF.Ln, [s_t], [s_t], bias=1.0)
                            stt('dve', s_t[:, 8:16], s_t[:, 0:8], -1.0, SAe, ALU.mult, ALU.mult, [s_t, sm], [s_t])
                            pb = P()
                            mm(pb[:, 0:8], TRI, s_t[:, 8:16], [cst, s_t], [pb])
                            mm(pb[:, 8:16], BLK, s_t[:, 8:16], [cst, s_t], [pb])
                            for c in range(2):
                                mm(pb[:, 16 + 8 * c:24 + 8 * c], ROWC(c), s_t[:, 8:16], [cst, s_t], [pb])
                            act(s_t[:, 16:24], pb[:, 0:8], AF.Copy, [pb], [s_t])
                            act(s_t[:, 24:32], pb[:, 0:8], AF.Exp, [pb], [s_t])
                            tt('dve', s_t[:, 48:56], pb[:, 8:16], s_t[:, 16:24], ALU.subtract, [pb, s_t], [s_t])
                            act(s_t[:, 48:56], s_t[:, 48:56], AF.Exp, [s_t], [s_t])
                            for c in range(2):
                                tsc('dve', s_t[:, 32 + 8 * c:40 + 8 * c], s_t[:, 48:56], CHSEL[:, c:c + 1], ALU.mult, [s_t, cst], [s_t])
                            act(s_bd[:], pb[:, 16:32], AF.Exp, [pb], [s_bd])
                            if SO:
                                tt('dve', s_tot[:], s_tot[:], pb[:, 16:24], ALU.add, [s_tot, pb], [s_tot])
                                tt('dve', s_tot[:], s_tot[:], pb[:, 24:32], ALU.add, [s_tot, pb], [s_tot])
                        def ssd_p2():
                            tt('dve', h8(s_y[:]), h8(cv[:, XO:XO + 512]), hb8(s_t[:, 0:8]), ALU.mult, [cv, s_t], [s_y])
                            if not SO:
                                act(s_xdt[:], s_y[:], AF.Copy, [s_y], [s_xdt])
                            for c in range(2):
                                tt('pool', h8(s_xde[c][:]), h8(s_y[:]), hb8(s_t[:, 32 + 8 * c:40 + 8 * c]), ALU.mult, [s_y, s_t], [s_xde[c]])
                            if not SO:
                                tt('pool', h8(s_xD[:]), h8(cv[:, XO:XO + 512]), hb8(SDD), ALU.mult, [cv, sm], [s_xD])
                            act(s_BCb[:], cv[:, BO:BO + 512], AF.Copy, [cv], [s_BCb])
                        def ssd_p3():
                            if SO:
                                return
                            pbt = P()
                            for j in range(4):
                                mm(pbt[:, j * 128:(j + 1) * 128], s_BCb[:, j * 128:(j + 1) * 128], identb[:], [s_BCb, identb], [pbt])
                            act(s_BCT[:], pbt[:], AF.Copy, [pbt], [s_BCT])
                            pbb = P()
                            for g_ in range(2):
                                mm(pbb[:, g_ * 128:(g_ + 1) * 128], s_BCT[:, g_ * 128:(g_ + 1) * 128],
                                   s_BCT[:, (2 + g_) * 128:(3 + g_) * 128], [s_BCT], [pbb])
                            tt('dve', s_BCm[:].rearrange("p (g i) -> p g i", g=2), pbb[:, 0:256].rearrange("p (g i) -> p g i", g=2),
                               TRI.unsqueeze(1).to_broadcast([128, 2, 128]), ALU.mult, [pbb, cst], [s_BCm])
                        def ssd_p4():
                            if SO:
                                return
                            tt('pool', s_Gs[:].rearrange("p (h j) -> p h j", h=8), SUPS.unsqueeze(1).to_broadcast([128, 8, 128]),
                               s_t[:, 8:16].unsqueeze(2).to_broadcast([128, 8, 128]), ALU.mult, [cst, s_t], [s_Gs])
                            for half in range(2):
                                pbD = P()
                                for hh in range(4):
                                    h = half * 4 + hh
                                    mm(pbD[:, hh * 128:(hh + 1) * 128], s_Gs[:, h * 128:(h + 1) * 128], TLE, [s_Gs, cst], [pbD])
                                act(s_E[:, half * 512:(half + 1) * 512], pbD[:], AF.Exp, [pbD], [s_E])
                                tt('dve', s_scT[:, half * 512:(half + 1) * 512].rearrange("p (h i) -> p h i", h=4),
                                   s_E[:, half * 512:(half + 1) * 512].rearrange("p (h i) -> p h i", h=4),
                                   s_BCm[:, half * 128:(half + 1) * 128].unsqueeze(1).to_broadcast([128, 4, 128]), ALU.mult,
                                   [s_E, s_BCm], [s_scT])
                            pyd = P()
                            for h in range(8):
                                mm(pyd[:, h * 64:(h + 1) * 64], s_scT[:, h * 128:(h + 1) * 128], s_xdt[:, h * 64:(h + 1) * 64],
                                   [s_scT, s_xdt], [pyd])
                            tt('dve', s_y[:], pyd[:], s_xD[:], ALU.add, [pyd, s_xD], [s_y])
                        TTg = neumann(g_LT, g_L, hooks=[ssd_p1, ssd_p2, ssd_p3, ssd_p4])
                        pbW = P()
                        for h in range(4):
                            mm(pbW[0:64, h * 128:(h + 1) * 128], g_kbg[:, h * 64:(h + 1) * 64], TTg[:, h * 128:(h + 1) * 128],
                               [g_kbg, TTg], [pbW])
                        act(g_nwT[:], pbW[0:64, :], AF.Copy, [pbW], [g_nwT], scale=-1.0)
                        for c in range(2):
                            cs_ = slice(c * 64, (c + 1) * 64)
                            if is_s:
                                s_id = 2 * (t - PT) + c
                                LD(gS[:].rearrange("k (h v) -> k h v", h=4), st_gdn[l, s_id].rearrange("h k v -> k h v"), [st_gdn], [gS])
                                act(gSb[:], gS[:], AF.Copy, [gS], [gSb])
                            pv = P()
                            for h in range(4):
                                vs = slice(h * 64, (h + 1) * 64)
                                ws = slice(h * W, (h + 1) * W)
                                rs_ = slice(h * W, h * W + 64)
                                mm(pv[:, ws], g_nwT[:, h * 128:(h + 1) * 128], gSb[:, ws], [g_nwT, gSb], [pv], start=True, stop=False)
                                mm(pv[:, rs_], TTg[:, h * 128:(h + 1) * 128], g_vb[:, vs], [TTg, g_vb], [pv], start=False, stop=True)
                            act(g_vn[cs_, :], pv[cs_, 0:4 * W], AF.Copy, [pv], [g_vn])
                            pS = P()
                            if not SO:
                                po = P()
                            for h in range(4):
                                vs = slice(h * 64, (h + 1) * 64)
                                ws = slice(h * W, (h + 1) * W)
                                if not SO:
                                    mm(po[:, vs], qdT(h), gSb[:, vs], [g_fT, gSb], [po], start=True, stop=False)
                                    mm(po[:, vs], g_attnT[:, h * 128:(h + 1) * 128], g_vn[:, vs], [g_attnT, g_vn], [po], start=False, stop=True)
                                mm(pS[0:64, ws], g_kd[c][:, vs], g_vn[:, ws], [g_kd[c], g_vn], [pS])
                            if not SO:
                                act(g_o[cs_, :], po[cs_, 0:256], AF.Copy, [po], [g_o])
                            tt('dve', gS[:].rearrange("k (h v) -> k h v", h=4), gS[:].rearrange("k (h v) -> k h v", h=4),
                               g_bd[:, 4 * c:4 * c + 4].unsqueeze(2).to_broadcast([64, 4, W]), ALU.mult, [gS, g_bd], [gS])
                            tt('dve', gS[:], gS[:], pS[0:64, 0:4 * W], ALU.add, [gS, pS], [gS])
                            act(gSb[:], gS[:], AF.Copy, [gS], [gSb])
                            if (not SO) and (is_s or (t == PT - 1 and c == 1)):
                                dst = o_sgdn[l, 2 * (t - PT) + c] if is_s else o_pgdn[l]
                                dbuf = o_sgdn if is_s else o_pgdn
                                ST(dst.rearrange("h k v -> k h v"), gS[:].rearrange("k (h v) -> k h v", h=4), [gS], [dbuf])
                        if not SO:
                            act(g_sq[:, 0:256], g_o[:], AF.Square, [g_o], [g_sq])
                            red(g_ss[:, 0:4], g_sq[:, 0:256].rearrange("p (h d) -> p h d", h=4), [g_sq], [g_ss])
                            rsqrt(g_ss[:, 4:8], g_ss[:, 0:4], [g_ss], [g_ss], epsb[:, 0:1], scale=1.0 / 64)
                            tt('dve', h3(g_o[:]), h3(g_o[:]), hb(g_ss[:, 4:8]), ALU.mult, [g_o, g_ss], [g_o])
                            tt('pool', g_o[:], g_o[:], gnw[:], ALU.mult, [g_o, gnw], [g_o])
                            sigmoid(g_sq[:, 256:512], pz[:, 0:256], [pz], [g_sq])
                            tt('pool', g_o[:], g_o[:], pz[:, 0:256], ALU.mult, [g_o, pz], [g_o])
                            tt('dve', mix[:, 0:256], g_o[:], g_sq[:, 256:512], ALU.mult, [g_o, g_sq], [mix])

                        for c in range(2):
                            cs_ = slice(c * 64, (c + 1) * 64)
                            if is_s:
                                s_id = 2 * (t - PT) + c
                                LD(stio[:, :].rearrange("p (h n) -> p h n", h=8), st_ssm[l, s_id].rearrange("h p n -> p h n"), [st_ssm], [stio])
                                for half in range(2):
                                    pbs = P()
                                    for hh in range(4):
                                        h = half * 4 + hh
                                        mm(pbs[:, hh * 64:(hh + 1) * 64], stio[:, h * 128:(h + 1) * 128], IDF[0:64, 0:64], [stio, cst], [pbs])
                                    act(sS[:, half * 256:(half + 1) * 256], pbs[:, 0:256], AF.Copy, [pbs], [sS])
                                act(sSb[:], sS[:], AF.Copy, [sS], [sSb])
                            pcs = P()
                            if not SO:
                                pyo = P()
                            for g_ in range(2):
                                gs = slice(g_ * 256, (g_ + 1) * 256)
                                if not SO:
                                    mm(pyo[:, gs], s_BCT[:, (2 + g_) * 128:(3 + g_) * 128], sSb[:, gs], [s_BCT, sSb], [pyo])
                                mm(pcs[:, gs], s_BCb[:, g_ * 128:(g_ + 1) * 128], s_xde[c][:, gs], [s_BCb, s_xde[c]], [pcs])
                            if not SO:
                                tt('dve', h8(s_tmp[cs_, :]), h8(pyo[cs_, :]), s_t[cs_, 24:32].unsqueeze(2).to_broadcast([64, 8, 64]), ALU.mult,
                                   [pyo, s_t], [s_tmp])
                                tt('pool', s_y[cs_, :], s_y[cs_, :], s_tmp[cs_, :], ALU.add, [s_y, s_tmp], [s_y])
                            tt('dve', h8(sS[:]), h8(sS[:]), s_bd[:, 8 * c:8 * c + 8].unsqueeze(2).to_broadcast([128, 8, 64]), ALU.mult,
                               [sS, s_bd], [sS])
                            tt('dve', sS[:], sS[:], pcs[:], ALU.add, [sS, pcs], [sS])
                            act(sSb[:], sS[:], AF.Copy, [sS], [sSb])
                            if not SO:
                                if is_s or (t == PT - 1 and c == 1):
                                    dst = o_sssm[l, 2 * (t - PT) + c] if is_s else o_pssm[l]
                                    dbuf = o_sssm if is_s else o_pssm
                                    for half in range(2):
                                        pbs = P()
                                        for hh in range(4):
                                            h = half * 4 + hh
                                            mm(pbs[0:64, hh * 128:(hh + 1) * 128], sS[:, h * 64:(h + 1) * 64], IDF, [sS, cst], [pbs])
                                        act(stio[:, half * 512:(half + 1) * 512], pbs[0:64, :], AF.Copy, [pbs], [stio])
                                    ST(dst.rearrange("h p n -> p h n"), stio[:, :].rearrange("p (h n) -> p h n", h=8), [stio], [dbuf])
                        if not SO:
                            sigmoid(s_tmp[:], pz[:, SZ - 768:SZ - 768 + 512], [pz], [s_tmp])
                            tt('pool', s_y[:], s_y[:], pz[:, SZ - 768:SZ - 768 + 512], ALU.mult, [s_y, pz], [s_y])
                            tt('dve', s_y[:], s_y[:], s_tmp[:], ALU.mult, [s_y, s_tmp], [s_y])
                            act(s_tmp[:], s_y[:], AF.Square, [s_y], [s_tmp])
                            red(s_ss[:, 0:2], s_tmp[:].rearrange("p (g d) -> p g d", g=2), [s_tmp], [s_ss])
                            rsqrt(s_ss[:, 2:4], s_ss[:, 0:2], [s_ss], [s_ss], epsb[:, 0:1], scale=1.0 / 256)
                            tt('dve', s_y[:].rearrange("p (g d) -> p g d", g=2), s_y[:].rearrange("p (g d) -> p g d", g=2),
                               s_ss[:, 2:4].unsqueeze(2).to_broadcast([128, 2, 256]), ALU.mult, [s_y, s_ss], [s_y])
                            tt('pool', mix[:, 256:768], s_y[:], snw[:], ALU.mult, [s_y, snw], [mix])

                        for c in range(2):
                            if t < PT and c == 1:
                                continue
                            r0 = seq_rows(t, c)
                            nr = 128 if t < PT else 64
                            LD(prv[c * 64:c * 64 + nr, :], projs[r0 - 1:r0 - 1 + nr, RR:RR + 1024], [projs], [prv])
                        pc = pr[:, 8:1032]
                        tt('pool', prv[:], prv[:], pc, ALU.subtract, [prv, pr], [prv])
                        tt('pool', prv[:], prv[:], mut[:], ALU.mult, [prv, mut], [prv])
                        tt('dve', r_xm[:], prv[:], pc, ALU.add, [prv, pr], [r_xm])
                        Rr, Rk, Rv = r_xm[:, 0:256], r_xm[:, 256:512], r_xm[:, 512:768]
                        sigmoid(r_tmp[:, 0:64], r_xm[:, 768:832], [r_xm], [r_tmp], scale=2.0)
                        tsc('dve', r_lin[:, 0:64], r_tmp[:, 0:64], 2.0, ALU.mult, [r_tmp], [r_lin], s2=-1.0, op1=ALU.add)
                        act(r_lin[:, 64:128], r_xm[:, 832:896], AF.Copy, [r_xm], [r_lin])
                        if not SO:
                            sigmoid(r_tmp[:, 64:192], r_xm[:, 896:1024], [r_xm], [r_tmp])
                            act(r_lin[:, 128:256], r_tmp[:, 64:192], AF.Copy, [r_tmp], [r_lin])
                        pbt = P()
                        mm(pbt[0:64, 0:128], r_lin[:, 0:64], identb[:], [r_lin, identb], [pbt])
                        mm(pbt[0:64, 128:256], r_lin[:, 64:128], identb[:], [r_lin, identb], [pbt])
                        if not SO:
                            mm(pbt[:, 256:384], r_lin[:, 128:256], identb[:], [r_lin, identb], [pbt])
                        r_linT2 = r_linT
                        act(r_linT[0:64, :], pbt[0:64, 0:256], AF.Copy, [pbt], [r_linT])
                        pl = P()
                        if not SO:
                            act(g_sq[:, 0:128], pbt[:, 256:384], AF.Copy, [pbt], [g_sq])
                            act(r_at[:, 0:128], g_sq[:, 0:128], AF.Copy, [g_sq], [r_at])
                            pl2 = P()
                        mm(pl[:, 0:256], r_linT[0:64, 0:128], lb[0:64, 0, :], [r_linT, lb], [pl])
                        mm(pl[:, 256:512], r_linT[0:64, 128:256], lb[0:64, 1, :], [r_linT, lb], [pl])
                        if not SO:
                            mm(pl2[:, 0:256], r_at[:, 0:128], lb[:, 2, :], [r_at, lb], [pl2])
                        tt('dve', r_tmp[:], pl[:, 0:256], W0, ALU.add, [pl, rv], [r_tmp])
                        sigmoid(r_lw[:], r_tmp[:], [r_tmp], [r_lw])
                        tsc('dve', r_lw[:], r_lw[:], -0.6065306597126334, ALU.mult, [r_lw], [r_lw])
                        tt('dve', r_tmp[:], pl[:, 256:512], A0, ALU.add, [pl, rv], [r_tmp])
                        sigmoid(r_icl[:], r_tmp[:], [r_tmp], [r_icl])
                        if not SO:
                            act(r_gate[:], pl2[:, 0:256], AF.Copy, [pl2], [r_gate])
                        tt('pool', r_kk[:], Rk, KK_, ALU.mult, [r_xm, rv], [r_kk])
                        act(r_tmp[:], r_kk[:], AF.Square, [r_kk], [r_tmp])
                        red(r_ss[:, 0:4], h3(r_tmp[:]), [r_tmp], [r_ss])
                        rsqrt(r_ss[:, 4:8], r_ss[:, 0:4], [r_ss], [r_ss], epsb[:, 0:1])
                        tt('dve', h3(r_kk[:]), h3(r_kk[:]), hb(r_ss[:, 4:8]), ALU.mult, [r_kk, r_ss], [r_kk])
                        stt('dve', r_tmp[:], r_icl[:], -1.0, KA_, ALU.add, ALU.mult, [r_icl, rv], [r_tmp])
                        stt('dve', r_k2[:], r_tmp[:], 1.0, Rk, ALU.add, ALU.mult, [r_tmp, r_xm], [r_k2])
                        pcs = P()
                        mm(pcs[:, 0:256], TRI, r_lw[:], [cst, r_lw], [pcs])
                        mm(pcs[:, 256:512], BLK, r_lw[:], [cst, r_lw], [pcs])
                        pg = P()
                        for h in range(4):
                            mm(pg[0:64, 2 * h:2 * h + 2], r_lw[:, h * 64:(h + 1) * 64], CHSEL, [r_lw, cst], [pg])
                        act(r_gC[:], pg[0:64, 0:8], AF.Exp, [pg], [r_gC])
                        act(r_cs[:], pcs[:, 0:256], AF.Copy, [pcs], [r_cs])
                        if not SO:
                            act(r_e[:], r_cs[:], AF.Exp, [r_cs], [r_e])
                            tt('dve', r_rt[:], Rr, r_e[:], ALU.mult, [r_xm, r_e], [r_rt])
                        tt('dve', r_tmp[:], r_cs[:], r_lw[:], ALU.subtract, [r_cs, r_lw], [r_tmp])
                        act(r_e[:], r_tmp[:], AF.Exp, [r_tmp], [r_e])
                        stt('dve', r_at[:], r_kk[:], -1.0, r_e[:], ALU.mult, ALU.mult, [r_kk, r_e], [r_at])
                        tt('pool', r_tmp2[:], r_kk[:], r_icl[:], ALU.mult, [r_kk, r_icl], [r_tmp2])
                        act(r_e[:], r_cs[:], AF.Exp, [r_cs], [r_e], scale=-1.0)
                        tt('dve', r_bt[:], r_tmp2[:], r_e[:], ALU.mult, [r_tmp2, r_e], [r_bt])
                        tt('pool', r_kt[:], r_k2[:], r_e[:], ALU.mult, [r_k2, r_e], [r_kt])
                        tt('dve', r_tmp[:], pcs[:, 256:512], r_cs[:], ALU.subtract, [pcs, r_cs], [r_tmp])
                        act(r_e[:], r_tmp[:], AF.Exp, [r_tmp], [r_e])
                        for c in range(2):
                            stt('dve', r_bh[c][:], r_tmp2[:], CHSEL[:, c:c + 1], r_e[:], ALU.mult, ALU.mult, [r_tmp2, cst, r_e], [r_bh[c]])
                            stt('dve', r_kh[c][:], r_k2[:], CHSEL[:, c:c + 1], r_e[:], ALU.mult, ALU.mult, [r_k2, cst, r_e], [r_kh[c]])
                        act(r_vb[:], Rv, AF.Copy, [r_xm], [r_vb])
                        fTr = r_fT[:].rearrange("p (h w i) -> p h w i", h=4, w=4)
                        for w_, src in enumerate([r_at, r_rt, r_bt, r_kt]):
                            if SO and w_ == 1:
                                continue
                            pbt = transpose_heads(None, None, src, 0, src)
                            act(fTr[:, :, w_, :], pbt[0:64, :].rearrange("p (h i) -> p h i", h=4), AF.Copy, [pbt], [r_fT])
                        for half in range(2):
                            pab = P(); pak = P()
                            for hh in range(2):
                                h = half * 2 + hh
                                ar = fTr[:, h, 0:2, :]
                                mm(pab[:, hh * 256:(hh + 1) * 256], fTr[:, h, 2, :], ar, [r_fT], [pab])
                                mm(pak[:, hh * 256:(hh + 1) * 256], fTr[:, h, 3, :], ar, [r_fT], [pak])
                            for (pb_, dst_) in ((pab, r_AB), (pak, r_AK)):
                                dv = dst_[:, half * 512:(half + 1) * 512].rearrange("p (h w i) -> p h w i", h=2, w=2)
                                sv = pb_[:].rearrange("p (h w i) -> p h w i", h=2, w=2)
                                tt('dve', dv[:, :, 0, :], sv[:, :, 0, :], MSTR.unsqueeze(1).to_broadcast([128, 2, 128]), ALU.mult, [pb_, cst], [dst_])
                                if not SO:
                                    tt('dve', dv[:, :, 1, :], sv[:, :, 1, :], TRI.unsqueeze(1).to_broadcast([128, 2, 128]), ALU.mult, [pb_, cst], [dst_])
                        ABv = r_AB[:].rearrange("p (h w i) -> p h w i", h=4, w=2)
                        AKv = r_AK[:].rearrange("p (h w i) -> p h w i", h=4, w=2)
                        act(r_A0[:].rearrange("p (h i) -> p h i", h=4), ABv[:, :, 0, :], AF.Copy, [r_AB], [r_A0])
                        pbT = P()
                        for h in range(4):
                            hs = slice(h * 128, (h + 1) * 128)
                            mm(pbT[:, hs], r_A0[:, hs], identb[:], [r_A0, identb], [pbT])
                        act(r_A0T[:], pbT[:], AF.Copy, [pbT], [r_A0T])
                        TTr = neumann(r_A0, r_A0T)
                        for c in range(2):
                            cs_ = slice(c * 64, (c + 1) * 64)
                            if is_s:
                                s_id = 2 * (t - PT) + c
                                LD(stio[:, 0:256].rearrange("p (h k) -> p h k", h=4), st_rwkv[l, s_id].rearrange("h v k -> v h k"), [st_rwkv], [stio])
                                pbs = P()
                                for h in range(4):
                                    mm(pbs[0:64, h * 64:(h + 1) * 64], stio[:, h * 64:(h + 1) * 64], IDF[0:64, 0:64], [stio, cst], [pbs])
                                act(rH[:], pbs[0:64, 0:256], AF.Copy, [pbs], [rH])
                                act(rHb[:], rH[:], AF.Copy, [rH], [rHb])
                            px = P()
                            for h in range(4):
                                vs = slice(h * 64, (h + 1) * 64)
                                ws = slice(h * W, (h + 1) * W)
                                rs_ = slice(h * W, h * W + 64)
                                mm(px[:, ws], fTr[:, h, 0, :], rHb[:, ws], [r_fT, rHb], [px], start=True, stop=False)
                                mm(px[:, rs_], AKv[:, h, 0, :], r_vb[:, vs], [r_AK, r_vb], [px], start=False, stop=True)
                            act(r_X[cs_, :], px[cs_, 0:4 * W], AF.Copy, [px], [r_X])
                            pu = P()
                            for h in range(4):
                                ws = slice(h * W, (h + 1) * W)
                                mm(pu[:, ws], TTr[:, h * 128:(h + 1) * 128], r_X[:, ws], [TTr, r_X], [pu])
                            act(r_U[cs_, :], pu[cs_, 0:4 * W], AF.Copy, [pu], [r_U])
                            ph = P()
                            if not SO:
                                py = P()
                            for h in range(4):
                                vs = slice(h * 64, (h + 1) * 64)
                                ws = slice(h * W, (h + 1) * W)
                                rs_ = slice(h * W, h * W + 64)
                                if not SO:
                                    mm(py[:, vs], fTr[:, h, 1, :], rHb[:, vs], [r_fT, rHb], [py], start=True, stop=False)
                                    mm(py[:, vs], ABv[:, h, 1, :], r_U[:, vs], [r_AB, r_U], [py], start=False, stop=False)
                                    mm(py[:, vs], AKv[:, h, 1, :], r_vb[:, vs], [r_AK, r_vb], [py], start=False, stop=True)
                                mm(ph[0:64, ws], r_bh[c][:, vs], r_U[:, ws], [r_bh[c], r_U], [ph], start=True, stop=False)
                                mm(ph[0:64, rs_], r_kh[c][:, vs], r_vb[:, vs], [r_kh[c], r_vb], [ph], start=False, stop=True)
                            if not SO:
                                act(r_y[cs_, :], py[cs_, 0:256], AF.Copy, [py], [r_y])
                            gCv = r_gC[:].rearrange("k (h c) -> k h c", h=4)
                            tt('dve', rH[:].rearrange("k (h v) -> k h v", h=4), rH[:].rearrange("k (h v) -> k h v", h=4),
                               gCv[:, :, c:c + 1].to_broadcast([64, 4, W]), ALU.mult, [rH, r_gC], [rH])
                            tt('dve', rH[:], rH[:], ph[0:64, 0:4 * W], ALU.add, [rH, ph], [rH])
                            act(rHb[:], rH[:], AF.Copy, [rH], [rHb])
                            if not SO:
                                if is_s or (t == PT - 1 and c == 1):
                                    dst = o_srwkv[l, 2 * (t - PT) + c] if is_s else o_prwkv[l]
                                    dbuf = o_srwkv if is_s else o_prwkv
                                    pbs = P()
                                    for h in range(4):
                                        mm(pbs[0:64, h * 64:(h + 1) * 64], rH[:, h * 64:(h + 1) * 64], IDF[0:64, 0:64], [rH, cst], [pbs])
                                    act(stio[:, 512:768], pbs[0:64, 0:256], AF.Copy, [pbs], [stio])
                                    ST(dst.rearrange("h v k -> v h k"), stio[:, 512:768].rearrange("p (h k) -> p h k", h=4), [stio], [dbuf])
                        if not SO:
                            red(r_ss[:, 0:4], h3(r_y[:]), [r_y], [r_ss])
                            tsc('dve', r_ss[:, 0:4], r_ss[:, 0:4], -1.0 / 64, ALU.mult, [r_ss], [r_ss])
                            tt('dve', h3(r_y[:]), h3(r_y[:]), hb(r_ss[:, 0:4]), ALU.add, [r_y, r_ss], [r_y])
                            act(r_tmp[:], r_y[:], AF.Square, [r_y], [r_tmp])
                            red(r_ss[:, 4:8], h3(r_tmp[:]), [r_tmp], [r_ss])
                            rsqrt(r_ss[:, 8:12], r_ss[:, 4:8], [r_ss], [r_ss], epsb[:, 1:2], scale=1.0 / 64)
                            tt('dve', h3(r_y[:]), h3(r_y[:]), hb(r_ss[:, 8:12]), ALU.mult, [r_y, r_ss], [r_y])
                            tt('pool', r_y[:], r_y[:], LNW, ALU.mult, [r_y, rv], [r_y])
                            tt('pool', r_y[:], r_y[:], LNB, ALU.add, [r_y, rv], [r_y])
                            tt('pool', r_tmp[:], Rr, r_k2[:], ALU.mult, [r_xm, r_k2], [r_tmp])
                            tt('pool', r_tmp[:], r_tmp[:], RK_, ALU.mult, [r_tmp, rv], [r_tmp])
                            red(r_ss[:, 12:16], h3(r_tmp[:]), [r_tmp], [r_ss])
                            tt('dve', h3(r_tmp[:]), h3(Rv), hb(r_ss[:, 12:16]), ALU.mult, [r_xm, r_ss], [r_tmp])
                            tt('dve', r_y[:], r_y[:], r_tmp[:], ALU.add, [r_y, r_tmp], [r_y])
                            tt('dve', mix[:, 768:1024], r_y[:], r_gate[:], ALU.mult, [r_y, r_gate], [mix])

                            act(mixb[:], mix[:], AF.Copy, [mix], [mixb])
                            for half in range(2):
                                pb = P()
                                for kk in range(4):
                                    kc = half * 4 + kk
                                    mm(pb[:, kk * 128:(kk + 1) * 128], mixb[:, kc * 128:(kc + 1) * 128], identb[:], [mixb, identb], [pb])
                                act(mixT[:, half * 4:(half + 1) * 4, :], pb[:].rearrange("p (k i) -> p k i", k=4), AF.Copy, [pb], [mixT])
                            ST(mixTs[t], mixT[:], [mixT], [mixTs])
                    if SO:
                        ST(agin_v[:, 0:512], sS[:], [sS], [agin_b])
                        ST(agin_v[:, 512:520], s_tot[:], [s_tot], [agin_b])
                        for (st_, c0_, j_) in ((gS, 520, 0), (rH, 1032, 1)):
                            ST(agin_v[0:64, c0_:c0_ + 256].rearrange("k (h v) -> k h v", h=4),
                               st_[:].rearrange("k (h w) -> k h w", h=4)[:, :, 0:64], [st_], [agin_b])
                            pbm = P()
                            for h in range(4):
                                mm(pbm[0:64, h * 64:(h + 1) * 64], st_[:, h * W + 64:(h + 1) * W], IDF[0:64, 0:64], [st_, cst], [pbm])
                            act(stio[:, j_ * 256:(j_ + 1) * 256], pbm[0:64, 0:256], AF.Copy, [pbm], [stio])
                            ST(agin_v[0:64, c0_ + 256:c0_ + 512], stio[:, j_ * 256:(j_ + 1) * 256], [stio], [agin_b])
                        if os.environ.get('NO_CC'):
                            for r_ in range(8):
                                ST(agout[r_ * 128:(r_ + 1) * 128, :], agin[:], [agin_b], [agout_b])
                        else:
                            em.coll(ccs_l[l], [agin_b], [agout_b],
                                    lambda e: e.collective_compute("AllGather", ALU.bypass, replica_groups=[list(range(8))],
                                                                   ins=[agin_b.t.ap().opt()], outs=[agout_b.t.ap().opt()]))
                    em.end_phase(em.scope_bufs.get(id(mes), []))

            def combine():
                with ExitStack() as ces:
                    agall = em.sb("agall", [128, 8, AGW], F32, ces)
                    for r in range(8):
                        LD(agall[:, r, :], agout_v[r * 128:(r + 1) * 128, :], [agout_b], [agall])
                    oh = em.sb("oh", [128, 8], F32, ces)
                    LD(oh[:], oh_me[:], [oh_me], [oh])
                    dtot = em.sb("dtot", [128, 8, 8], F32, ces)
                    act(dtot[:], agall[:, :, 512:520], AF.Exp, [agall], [dtot])
                    gcur = em.sb("gcur", [64, 256], F32, ces); rcur = em.sb("rcur", [64, 256], F32, ces)
                    scur = em.sb("scur", [128, 512], F32, ces)
                    for b_ in (gcur, rcur, scur, gSin, rSin, sSin):
                        em.op('pool', lambda e, b_=b_: e.memset(b_[:], 0.0), [], [b_])
                    for r in range(8):
                        stt('dve', gSin[:], gcur[:], oh[0:64, r:r + 1], gSin[:], ALU.mult, ALU.add, [gcur, oh, gSin], [gSin])
                        stt('dve', rSin[:], rcur[:], oh[0:64, r:r + 1], rSin[:], ALU.mult, ALU.add, [rcur, oh, rSin], [rSin])
                        stt('dve', sSin[:], scur[:], oh[:, r:r + 1], sSin[:], ALU.mult, ALU.add, [scur, oh, sSin], [sSin])
                        if r == 7:
                            break
                        pbg = P()
                        for h in range(4):
                            hs_ = slice(h * 64, (h + 1) * 64)
                            mm(pbg[0:64, h * 64:(h + 1) * 64], agall[0:64, r, 776 + h * 64:776 + (h + 1) * 64], gcur[:, hs_],
                               [agall, gcur], [pbg])
                            mm(pbg[0:64, 256 + h * 64:256 + (h + 1) * 64], agall[0:64, r, 1288 + h * 64:1288 + (h + 1) * 64], rcur[:, hs_],
                               [agall, rcur], [pbg])
                        tt('dve', gcur[:], pbg[0:64, 0:256], agall[0:64, r, 520:776], ALU.add, [pbg, agall], [gcur])
                        tt('dve', rcur[:], pbg[0:64, 256:512], agall[0:64, r, 1032:1288], ALU.add, [pbg, agall], [rcur])
                        tt('dve', scur[:].rearrange("p (h d) -> p h d", h=8), scur[:].rearrange("p (h d) -> p h d", h=8),
                           dtot[:, r, :].unsqueeze(2).to_broadcast([128, 8, 64]), ALU.mult, [scur, dtot], [scur])
                        tt('dve', scur[:], scur[:], agall[:, r, 0:512], ALU.add, [scur, agall], [scur])
                    em.end_phase(em.scope_bufs.get(id(ces), []))

            if shard:
                phase_M("state")
                combine()
            elif l == 0:
                for b_ in (gSin, rSin, sSin):
                    em.op('pool', lambda e, b_=b_: e.memset(b_[:], 0.0), [], [b_])
            phase_M("full")

            with ExitStack() as bes:
                woutr = em.sb("woutr", [128, 8, D], BF16, bes)
                wgr = em.sb("wgr", [128, 8, FH], BF16, bes)
                wur = em.sb("wur", [128, 8, FH], BF16, bes)
                wdr = em.sb("wdr", [128, NJ, D], BF16, bes)
                for kk in range(8):
                    LD(woutr[:, kk, :], wout_b[l, kk * 128:(kk + 1) * 128, :], [wout_b], [woutr])
                    LD(wgr[:, kk, :], wg_b[l, kk * 128:(kk + 1) * 128, :], [wg_b], [wgr])
                    LD(wur[:, kk, :], wu_b[l, kk * 128:(kk + 1) * 128, :], [wu_b], [wur])
                for j in range(NJ):
                    LD(wdr[:, j, :], wd_b[l, j * 128:(j + 1) * 128, :], [wd_b], [wdr])
                n2w = em.sb("n2w", [128, 8], F32, bes)
                LD(n2w[:], norm2_w[l], [norm2_w], [n2w])
                fnw = em.sb("fnw", [128, D], F32, bes)
                LD(fnw[:], final_norm_w[0:1, :].partition_broadcast(128), [], [fnw])
                xt = [em.sb("bxt%d" % i, [128, D], F32, bes) for i in range(2)]
                mT = [em.sb("bmT%d" % i, [128, 8, 128], BF16, bes) for i in range(2)]
                junk = em.sb("bjunk", [128, D], F32, bes)
                ssq = em.sb("bssq", [128, 4], F32, bes)
                hb2 = em.sb("bhb", [128, D], BF16, bes)
                h2T = em.sb("bh2T", [128, 8, 128], BF16, bes)
                sg = em.sb("bsg", [128, 512], F32, bes)
                ffT = em.sb("bffT", [128, NJ, 128], BF16, bes)
                for t in range(NT):
                    q = t % 2
                    LD(xt[q][:], xsrc[t * 128:(t + 1) * 128, :], [xsrc], [xt[q]])
                    LD(mT[q][:], mixTs[t], [mixTs], [mT[q]])
                    for cb_ in range(2):
                        pb = P()
                        for kc in range(8):
                            mm(pb[:], mT[q][:, kc, :], woutr[:, kc, cb_ * 512:(cb_ + 1) * 512], [mT[q], woutr], [pb],
                               start=(kc == 0), stop=(kc == 7))
                        tt('dve', xt[q][:, cb_ * 512:(cb_ + 1) * 512], xt[q][:, cb_ * 512:(cb_ + 1) * 512], pb[:], ALU.add,
                           [xt[q], pb], [xt[q]])
                    act(junk[:], xt[q][:], AF.Square, [xt[q]], [junk, ssq], accum=ssq[:, 0:1])
                    rsqrt(ssq[:, 1:2], ssq[:, 0:1], [ssq], [ssq], epsb[:, 0:1], scale=1.0 / D)
                    act(hb2[:], xt[q][:], AF.Copy, [xt[q], ssq], [hb2], scale=ssq[:, 1:2])
                    for half in range(2):
                        pb = P()
                        for kk in range(4):
                            kc = half * 4 + kk
                            mm(pb[:, kk * 128:(kk + 1) * 128], hb2[:, kc * 128:(kc + 1) * 128], identb[:], [hb2, identb], [pb])
                        for kk in range(4):
                            kc = half * 4 + kk
                            tsc('dve', h2T[:, kc, :], pb[:, kk * 128:(kk + 1) * 128], n2w[:, kc:kc + 1], ALU.mult, [pb, n2w], [h2T])
                    for j0 in range(0, NJ, 4):
                        nj = min(4, NJ - j0)
                        pg_ = P(); pu_ = P()
                        for jj in range(nj):
                            j = j0 + jj
                            for kc in range(8):
                                mm(pg_[:, jj * 128:(jj + 1) * 128], wgr[:, kc, j * 128:(j + 1) * 128], h2T[:, kc, :], [wgr, h2T], [pg_],
                                   start=(kc == 0), stop=(kc == 7))
                            for kc in range(8):
                                mm(pu_[:, jj * 128:(jj + 1) * 128], wur[:, kc, j * 128:(j + 1) * 128], h2T[:, kc, :], [wur, h2T], [pu_],
                                   start=(kc == 0), stop=(kc == 7))
                        w_ = nj * 128
                        sigmoid(sg[:, 0:w_], pg_[:, 0:w_], [pg_], [sg])
                        tt('dve', sg[:, 0:w_], sg[:, 0:w_], pg_[:, 0:w_], ALU.mult, [sg, pg_], [sg])
                        tt('dve', ffT[:, j0:j0 + nj, :], sg[:, 0:w_].rearrange("p (j i) -> p j i", j=nj),
                           pu_[:, 0:w_].rearrange("p (j i) -> p j i", j=nj), ALU.mult, [sg, pu_], [ffT])
                    for cb_ in range(2):
                        pb = P()
                        for j in range(NJ):
                            mm(pb[:], ffT[:, j, :], wdr[:, j, cb_ * 512:(cb_ + 1) * 512], [ffT, wdr], [pb],
                               start=(j == 0), stop=(j == NJ - 1))
                        tt('dve', xt[q][:, cb_ * 512:(cb_ + 1) * 512], xt[q][:, cb_ * 512:(cb_ + 1) * 512], pb[:], ALU.add,
                           [xt[q], pb], [xt[q]])
                    if l == 0:
                        ST(x1s[t * 128:(t + 1) * 128, :], xt[q][:], [xt[q]], [x1s])
                        if t == PT - 1 and shard:
                            ST(hin[:], xt[q][125:128, :], [xt[q]], [hin])
                    else:
                        act(junk[:], xt[q][:], AF.Square, [xt[q]], [junk, ssq], accum=ssq[:, 2:3])
                        rsqrt(ssq[:, 3:4], ssq[:, 2:3], [ssq], [ssq], epsb[:, 0:1], scale=1.0 / D)
                        stt('dve', xt[q][:], xt[q][:], ssq[:, 3:4], fnw[:], ALU.mult, ALU.mult, [xt[q], ssq, fnw], [xt[q]])
                        ST(y_out[t * 128:(t + 1) * 128, :], xt[q][:], [xt[q]], [y_out])
                if l == 0 and shard:
                    if os.environ.get('NO_CC'):
                        for r_ in range(8):
                            ST(hout[r_ * 3:(r_ + 1) * 3, :], hin[:], [hin], [hout])
                    else:
                        em.coll(ccs, [hin], [hout],
                                lambda e: e.collective_compute("AllGather", ALU.bypass, replica_groups=[list(range(8))],
                                                               ins=[hin.t.ap().opt()], outs=[hout.t.ap().opt()]))
                em.end_phase(em.scope_bufs.get(id(bes), []))
        em.barrier()
        print("instructions", em.n_ins, "waits", em.n_wait)
    return nc


_CACHE = {}


def kernel(**inp):
    f = lambda a: np.ascontiguousarray(np.asarray(a, dtype=np.float32))
    xp = f(inp['x_prompt'])[0]
    xsm = f(inp['x_sample'])
    NSH = 8 if SHARD else 1
    PTOK = xp.shape[0] // NSH
    if PTOK not in _CACHE:
        _CACHE[PTOK] = build(PTOK, SHARD)
    nc = _CACHE[PTOK]
    consts = host_consts()
    convw = np.concatenate([f(inp['gdn_conv_w']), f(inp['ssm_conv_w'])], axis=2)
    convb = np.concatenate([np.zeros((2, 768), np.float32), f(inp['ssm_conv_b'])], axis=1)
    gnw = np.tile(f(inp['gdn_norm_w']), (1, 4))
    rvec = np.stack([f(inp['rwkv_w0']), f(inp['rwkv_a0']), f(inp['rwkv_k_k']), f(inp['rwkv_k_a']),
                     f(inp['rwkv_r_k']).reshape(2, 256), f(inp['rwkv_ln_w']), f(inp['rwkv_ln_b'])], axis=1)
    common = {
        'consts': consts, 'norm1_w': f(inp['norm1_w']).reshape(2, 8, 128).transpose(0, 2, 1), 'w_in': f(inp['w_in']), 'convw': convw, 'convb': convb,
        'gdn_A_log': f(inp['gdn_A_log']), 'gdn_dt_bias': f(inp['gdn_dt_bias']), 'gdn_norm_w': gnw,
        'ssm_A_log': f(inp['ssm_A_log']), 'ssm_dt_bias': f(inp['ssm_dt_bias']), 'ssm_D': f(inp['ssm_D']),
        'ssm_norm_w': f(inp['ssm_norm_w']), 'rwkv_mu': f(inp['rwkv_mu']), 'rvec': rvec,
        'rwkv_w_up': f(inp['rwkv_w_up']), 'rwkv_a_up': f(inp['rwkv_a_up']), 'rwkv_g_up': f(inp['rwkv_g_up']),
        'w_out': f(inp['w_out']), 'norm2_w': f(inp['norm2_w']).reshape(2, 8, 128).transpose(0, 2, 1), 'ffn_w_gate': f(inp['ffn_w_gate']),
        'ffn_w_up': f(inp['ffn_w_up']), 'ffn_w_down': f(inp['ffn_w_down']),
        'final_norm_w': f(inp['final_norm_w']).reshape(1, D),
    }
    in_maps = []
    for c in range(8):
        sl = slice(c * NSEQ, (c + 1) * NSEQ)
        m = dict(common)
        cp = c if SHARD else 0
        m['xin'] = np.concatenate([xp[cp * PTOK:(cp + 1) * PTOK], xsm[sl].reshape(NSEQ * 64, D)], axis=0)
        xh = np.zeros((128, D), np.float32)
        if cp > 0:
            xh[0:3] = xp[c * PTOK - 3:c * PTOK]
        m['xhalo'] = xh
        oh = np.zeros((128, 8), np.float32)
        oh[:, c] = 1.0
        m['oh_me'] = oh
        sp = np.zeros((24, 128), np.float32)
        if cp > 0:
            for i_ in range(3):
                sp[3 * (c - 1) + i_, i_] = 1.0
        m['selprev'] = sp
        m['st_gdn'] = f(inp['state_gdn'])[:, sl]
        m['st_gdn_conv'] = f(inp['state_gdn_conv'])[:, sl]
        m['st_ssm'] = f(inp['state_ssm'])[:, sl]
        m['st_ssm_conv'] = f(inp['state_ssm_conv'])[:, sl]
        m['st_rwkv'] = f(inp['state_rwkv'])[:, sl]
        m['st_rwkv_shift'] = f(inp['state_rwkv_shift'])[:, sl]
        m = {k: np.ascontiguousarray(v) for k, v in m.items()}
        in_maps.append(m)
    res = run_bass_kernel_spmd(nc, in_maps, core_ids=list(range(8)))
    R = res.results
    y_prompt = np.concatenate([np.asarray(R[c]['y'])[0:PTOK] for c in range(NSH)], axis=0)[None]
    y_sample = np.concatenate([np.asarray(R[c]['y'])[PTOK:].reshape(NSEQ, 64, D) for c in range(8)], axis=0)
    cat = lambda k: np.concatenate([np.asarray(R[c][k]) for c in range(8)], axis=1)
    p1 = lambda k: np.asarray(R[NSH - 1][k])[:, None]
    outs = (y_prompt, y_sample,
            p1('o_pgdn'), p1('o_pgdn_conv'), p1('o_pssm'), p1('o_pssm_conv'), p1('o_prwkv'), p1('o_prwkv_shift'),
            cat('o_sgdn'), cat('o_sgdn_conv'), cat('o_sssm'), cat('o_sssm_conv'), cat('o_srwkv'), cat('o_srwkv_shift'))
    return tuple(np.ascontiguousarray(o, dtype=np.float32) for o in outs)
```
